# Optimizing a Trainium2 kernel written in Bass

```python
import jax, jax.numpy as jnp
from jax import lax
import numpy as np

D_MODEL = 1024
BATCH = 16
SEQ = 2048
DEPTH = 4

CTX_LEN = 256
GRID_W = 64
BLOCK = 128
WINDOW = 128

RET_HEADS = 4
RET_DK = 128
RET_DV = 128
RET_W = RET_HEADS * RET_DV

HEAD_DIM = 64
WIN_Q_HEADS = 8
WIN_KV_HEADS = 2
AX_Q_HEADS = 8
AX_KV_HEADS = 2
WIN_W = WIN_Q_HEADS * HEAD_DIM
AX_W = AX_Q_HEADS * HEAD_DIM

ROPE_BASE = 10000.0
EPS = 1e-6
NEG = -1e30

IN_SIZES = (RET_HEADS * RET_DK, RET_HEADS * RET_DK, RET_W, RET_W,
            WIN_W, WIN_KV_HEADS * HEAD_DIM, WIN_KV_HEADS * HEAD_DIM, WIN_W,
            AX_W, AX_KV_HEADS * HEAD_DIM, AX_KV_HEADS * HEAD_DIM, AX_W,
            3 * D_MODEL)
IN_W = sum(IN_SIZES)

kernel_name = 'hybrid_retention_window_axial_dit_block'


def rms_norm(x, w):
    xf = x.astype(jnp.float32)
    y = xf * lax.rsqrt(jnp.mean(xf * xf, axis=-1, keepdims=True) + EPS)
    return (y * w.astype(jnp.float32)).astype(x.dtype)


def heads(t, h):
    return t.reshape(*t.shape[:-1], h, t.shape[-1] // h)


def split_last(t, sizes):
    return jnp.split(t, np.cumsum(sizes)[:-1].tolist(), axis=-1)


def apply_rope(x, ang):
    d = x.shape[-1]
    xf = x.astype(jnp.float32).reshape(*x.shape[:-1], d // 2, 2)
    x0, x1 = xf[..., 0], xf[..., 1]
    cos = jnp.cos(ang)[None, :, None, :]
    sin = jnp.sin(ang)[None, :, None, :]
    out = jnp.stack([x0 * cos - x1 * sin, x0 * sin + x1 * cos], axis=-1)
    return out.reshape(x.shape).astype(x.dtype)


def axial_angles(n_tokens, dim):
    rows = n_tokens // GRID_W
    r, col = jnp.meshgrid(jnp.arange(rows, dtype=jnp.float32),
                          jnp.arange(GRID_W, dtype=jnp.float32), indexing='ij')
    quarter = dim // 4
    freqs = ROPE_BASE ** (-jnp.arange(quarter, dtype=jnp.float32) / quarter)
    return jnp.concatenate([r.reshape(-1)[:, None] * freqs[None],
                            col.reshape(-1)[:, None] * freqs[None]], axis=-1)


def retnet_angles(pos):
    theta = ROPE_BASE ** (-jnp.linspace(0.0, 1.0, RET_DK // 2, dtype=jnp.float32))
    return pos[:, None] * theta[None]


def retention_final_state(k, v, log_gamma):
    n = k.shape[1]
    w = jnp.exp((n - 1 - jnp.arange(n, dtype=jnp.float32))[:, None] * log_gamma[None])
    return jnp.einsum('bnhk,nh,bnhv->bhkv', k.astype(jnp.float32), w, v.astype(jnp.float32))


def retention_chunkwise(q, k, v, log_gamma, state0, strict):
    B, n, H, dk = q.shape
    dv = v.shape[-1]
    nc = n // BLOCK
    qc = q.astype(jnp.float32).reshape(B, nc, BLOCK, H, dk)
    kc = k.astype(jnp.float32).reshape(B, nc, BLOCK, H, dk)
    vc = v.astype(jnp.float32).reshape(B, nc, BLOCK, H, dv)
    idx = jnp.arange(BLOCK, dtype=jnp.float32)
    diff = idx[:, None] - idx[None, :]
    keep = (diff > 0) if strict else (diff >= 0)
    intra = jnp.where(keep[None], jnp.exp(jnp.where(keep, diff, 0.0)[None] * log_gamma[:, None, None]), 0.0)
    q_decay = jnp.exp((idx + 1.0)[None] * log_gamma[:, None])
    k_decay = jnp.exp((BLOCK - 1.0 - idx)[None] * log_gamma[:, None])
    chunk_decay = jnp.exp(BLOCK * log_gamma)
    s = jnp.einsum('bnihk,bnjhk->bnhij', qc, kc) * intra[None, None]
    o_intra = jnp.einsum('bnhij,bnjhv->bnihv', s, vc)
    chunk_kv = jnp.einsum('bnjhk,hj,bnjhv->nbhkv', kc, k_decay, vc)

    def step(R, kv):
        return chunk_decay[None, :, None, None] * R + kv, R

    _, R_prev = lax.scan(step, state0, chunk_kv)
    o_cross = jnp.einsum('bnihk,nbhkv,hi->bnihv', qc, R_prev, q_decay)
    return (o_intra + o_cross).reshape(B, n, H, dv)


def retention_branch(q_c, k_c, v_c, q_l, k_l, v_l, a_fwd, a_bwd, need_ctx):
    B, L, H, _ = k_c.shape
    S = k_l.shape[1]
    ang_c = retnet_angles(jnp.arange(L, dtype=jnp.float32))
    ang_l = retnet_angles(L + jnp.arange(S, dtype=jnp.float32))
    k_scale = RET_DK ** -0.5
    out_l = 0.0
    out_c = 0.0
    for a, backward in ((a_fwd, False), (a_bwd, True)):
        lg = jax.nn.log_sigmoid(a.astype(jnp.float32))
        flip = (lambda t: jnp.flip(t, axis=1)) if backward else (lambda t: t)
        qc1 = apply_rope(flip(q_c), ang_c)
        kc1 = apply_rope(flip(k_c), ang_c) * k_scale
        vc1 = flip(v_c)
        ql1 = apply_rope(flip(q_l), ang_l)
        kl1 = apply_rope(flip(k_l), ang_l) * k_scale
        vl1 = flip(v_l)
        state = retention_final_state(kc1, vc1, lg)
        out_l = out_l + flip(retention_chunkwise(ql1, kl1, vl1, lg, state, backward))
        if need_ctx:
            zeros = jnp.zeros((B, H, RET_DK, RET_DV), jnp.float32)
            out_c = out_c + flip(retention_chunkwise(qc1, kc1, vc1, lg, zeros, backward))

    def head_norm(o, dtype):
        o = o * lax.rsqrt(jnp.mean(o * o, axis=-1, keepdims=True) + EPS)
        return o.reshape(*o.shape[:2], RET_W).astype(dtype)

    o_l = head_norm(out_l, q_l.dtype)
    o_c = head_norm(out_c, q_c.dtype) if need_ctx else None
    return o_c, o_l


def gqa_attend(q, k, v, scale):
    s = jnp.einsum('bigrd,bjgd->bgrij', q, k).astype(jnp.float32) * scale
    p = jax.nn.softmax(s, axis=-1).astype(v.dtype)
    return jnp.einsum('bgrij,bjgd->bigrd', p, v)


def sink_softmax(sink, scores):
    col = jnp.broadcast_to(sink[:, :, None, None], scores[0].shape[:-1] + (1,))
    p = jax.nn.softmax(jnp.concatenate([col] + list(scores), axis=-1), axis=-1)[..., 1:]
    return split_last(p, [s.shape[-1] for s in scores])


def window_branch(q_c, k_c, v_c, q_l, k_l, v_l, sink, need_ctx):
    B, S, Hq, dh = q_l.shape
    G = k_l.shape[2]
    R = Hq // G
    L = k_c.shape[1]
    nb = S // BLOCK
    scale = dh ** -0.5
    sink_g = sink.astype(jnp.float32).reshape(G, R)
    ang = axial_angles(S, dh)
    q_l = apply_rope(q_l, ang)
    k_l = apply_rope(k_l, ang)
    qb = q_l.reshape(B, nb, BLOCK, G, R, dh)
    pad = ((0, 0), (BLOCK, BLOCK), (0, 0), (0, 0))
    kp = jnp.pad(k_l, pad).reshape(B, nb + 2, BLOCK, G, dh)
    vp = jnp.pad(v_l, pad).reshape(B, nb + 2, BLOCK, G, dh)
    kw = jnp.concatenate([kp[:, :-2], kp[:, 1:-1], kp[:, 2:]], axis=2)
    vw = jnp.concatenate([vp[:, :-2], vp[:, 1:-1], vp[:, 2:]], axis=2)
    blk = jnp.arange(nb)[:, None] * BLOCK
    qpos = blk + jnp.arange(BLOCK)[None]
    kpos = blk - BLOCK + jnp.arange(3 * BLOCK)[None]
    valid = ((jnp.abs(qpos[:, :, None] - kpos[:, None, :]) <= WINDOW)
             & (kpos >= 0)[:, None, :] & (kpos < S)[:, None, :])
    s_win = jnp.einsum('bnigrd,bnjgd->bngrij', qb, kw).astype(jnp.float32) * scale
    s_win = jnp.where(valid[None, :, None, None], s_win, NEG)
    s_ctx = jnp.einsum('bnigrd,bjgd->bngrij', qb, k_c).astype(jnp.float32) * scale
    p_ctx, p_win = sink_softmax(sink_g, [s_ctx, s_win])
    o = (jnp.einsum('bngrij,bjgd->bnigrd', p_ctx.astype(v_c.dtype), v_c)
         + jnp.einsum('bngrij,bnjgd->bnigrd', p_win.astype(vw.dtype), vw))
    o_l = o.reshape(B, S, Hq * dh)
    o_c = None
    if need_ctx:
        qcg = q_c.reshape(B, L, G, R, dh)
        s_cc = jnp.einsum('bigrd,bjgd->bgrij', qcg, k_c).astype(jnp.float32) * scale
        (p_cc,) = sink_softmax(sink_g, [s_cc])
        o_c = jnp.einsum('bgrij,bjgd->bigrd', p_cc.astype(v_c.dtype), v_c).reshape(B, L, Hq * dh)
    return o_c, o_l


def axial_branch(q_c, k_c, v_c, q_l, k_l, v_l, q_gain, k_gain, need_ctx):
    B, S, Hq, dh = q_l.shape
    G = k_l.shape[2]
    R = Hq // G
    L = k_c.shape[1]
    nb = S // BLOCK
    scale = dh ** -0.5
    ang = axial_angles(S, dh)
    q_l = apply_rope(rms_norm(q_l, q_gain), ang)
    k_l = apply_rope(rms_norm(k_l, k_gain), ang)
    k_c = rms_norm(k_c, k_gain)
    k_all = jnp.concatenate([k_c, k_l], axis=1)
    v_all = jnp.concatenate([v_c, v_l], axis=1)
    qb = jnp.moveaxis(q_l.reshape(B, nb, BLOCK, G, R, dh), 1, 0)
    o = lax.map(lambda qblk: gqa_attend(qblk, k_all, v_all, scale), qb)
    o_l = jnp.moveaxis(o, 0, 1).reshape(B, S, Hq * dh)
    o_c = None
    if need_ctx:
        qcg = rms_norm(q_c, q_gain).reshape(B, L, G, R, dh)
        o_c = gqa_attend(qcg, k_c, v_c, scale).reshape(B, L, Hq * dh)
    return o_c, o_l


def merge_branches(o_ret, g_ret, o_win, g_win, o_ax, g_ax, merge_logits, w_pr, w_pw, w_pa, w_out):
    b_ret = (o_ret * jax.nn.silu(g_ret)) @ w_pr
    b_win = (o_win * jax.nn.silu(g_win)) @ w_pw
    b_ax = (o_ax * jax.nn.silu(g_ax)) @ w_pa
    m_ret, m_win, m_ax = jnp.split(jax.nn.sigmoid(merge_logits), 3, axis=-1)
    return (m_ret * b_ret + m_win * b_win + m_ax * b_ax) @ w_out


def layer(x, xc, c, c_ctx, norm_w, w_mod, b_mod, w_in, a_fwd, a_bwd, sink, q_gain, k_gain,
          w_pr, w_pw, w_pa, w_out, need_ctx):
    shift, scale, gate = jnp.split(jax.nn.silu(c) @ w_mod + b_mod, 3, axis=-1)
    shift_c, scale_c, gate_c = jnp.split(jax.nn.silu(c_ctx) @ w_mod + b_mod, 3, axis=-1)
    h = rms_norm(x, norm_w) * (1.0 + scale[:, None]) + shift[:, None]
    hc = rms_norm(xc, norm_w) * (1.0 + scale_c) + shift_c
    rq, rk, rv, rg, wq, wk, wv, wg, aq, ak, av, ag, mg = split_last(h @ w_in, IN_SIZES)
    crq, crk, crv, crg, cwq, cwk, cwv, cwg, caq, cak, cav, cag, cmg = split_last(hc @ w_in, IN_SIZES)
    o_ret_c, o_ret = retention_branch(heads(crq, RET_HEADS), heads(crk, RET_HEADS), heads(crv, RET_HEADS),
                                      heads(rq, RET_HEADS), heads(rk, RET_HEADS), heads(rv, RET_HEADS),
                                      a_fwd, a_bwd, need_ctx)
    o_win_c, o_win = window_branch(heads(cwq, WIN_Q_HEADS), heads(cwk, WIN_KV_HEADS), heads(cwv, WIN_KV_HEADS),
                                   heads(wq, WIN_Q_HEADS), heads(wk, WIN_KV_HEADS), heads(wv, WIN_KV_HEADS),
                                   sink, need_ctx)
    o_ax_c, o_ax = axial_branch(heads(caq, AX_Q_HEADS), heads(cak, AX_KV_HEADS), heads(cav, AX_KV_HEADS),
                                heads(aq, AX_Q_HEADS), heads(ak, AX_KV_HEADS), heads(av, AX_KV_HEADS),
                                q_gain, k_gain, need_ctx)
    x = x + gate[:, None] * merge_branches(o_ret, rg, o_win, wg, o_ax, ag, mg, w_pr, w_pw, w_pa, w_out)
    if need_ctx:
        xc = xc + gate_c * merge_branches(o_ret_c, crg, o_win_c, cwg, o_ax_c, cag, cmg, w_pr, w_pw, w_pa, w_out)
    return x, xc


def setup_inputs(seed: int = 0) -> dict:
    key = jax.random.key(seed)
    ks = jax.random.split(key, 20)
    f32 = jnp.float32
    nrm = lambda k, shp: jax.random.normal(k, shp, f32)
    base_decay = 1.0 - 2.0 ** (-5.0 - jnp.arange(RET_HEADS, dtype=f32))
    base_logit = jnp.log(base_decay / (1.0 - base_decay))
    return {
        'x': nrm(ks[0], (BATCH, SEQ, D_MODEL)),
        'c': nrm(ks[1], (BATCH, D_MODEL)),
        'ctx': nrm(ks[2], (BATCH, CTX_LEN, D_MODEL)),
        'c_ctx': nrm(ks[3], (D_MODEL,)),
        'norm_w': 1.0 + 0.02 * nrm(ks[4], (DEPTH, D_MODEL)),
        'w_mod': nrm(ks[5], (DEPTH, D_MODEL, 3 * D_MODEL)) * (0.5 * D_MODEL ** -0.5),
        'b_mod': 0.01 * nrm(ks[6], (DEPTH, 3 * D_MODEL)),
        'w_in': nrm(ks[7], (DEPTH, D_MODEL, IN_W)) * D_MODEL ** -0.5,
        'ret_decay_fwd': base_logit[None] + 0.05 * nrm(ks[8], (DEPTH, RET_HEADS)),
        'ret_decay_bwd': base_logit[None] + 0.05 * nrm(ks[9], (DEPTH, RET_HEADS)),
        'win_sink': 0.5 * nrm(ks[10], (DEPTH, WIN_Q_HEADS)),
        'ax_q_gain': 1.0 + 0.02 * nrm(ks[11], (DEPTH, HEAD_DIM)),
        'ax_k_gain': 1.0 + 0.02 * nrm(ks[12], (DEPTH, HEAD_DIM)),
        'w_proj_ret': nrm(ks[13], (DEPTH, RET_W, D_MODEL)) * RET_W ** -0.5,
        'w_proj_win': nrm(ks[14], (DEPTH, WIN_W, D_MODEL)) * WIN_W ** -0.5,
        'w_proj_ax': nrm(ks[15], (DEPTH, AX_W, D_MODEL)) * AX_W ** -0.5,
        'w_out': nrm(ks[16], (DEPTH, D_MODEL, D_MODEL)) * D_MODEL ** -0.5,
        'final_norm_w': 1.0 + 0.02 * nrm(ks[17], (D_MODEL,)),
    }


def reference(x, c, ctx, c_ctx, norm_w, w_mod, b_mod, w_in, ret_decay_fwd, ret_decay_bwd, win_sink,
              ax_q_gain, ax_k_gain, w_proj_ret, w_proj_win, w_proj_ax, w_out, final_norm_w):
    xc = ctx
    for l in range(DEPTH):
        x, xc = layer(x, xc, c, c_ctx, norm_w[l], w_mod[l], b_mod[l], w_in[l],
                      ret_decay_fwd[l], ret_decay_bwd[l], win_sink[l], ax_q_gain[l], ax_k_gain[l],
                      w_proj_ret[l], w_proj_win[l], w_proj_ax[l], w_out[l], l < DEPTH - 1)
    return rms_norm(x, final_norm_w)
```

```python
import contextlib
import numpy as np
import concourse.bass as bass
import concourse.mybir as mybir
from concourse.bass_utils import run_bass_kernel_spmd

F32 = mybir.dt.float32
BF16 = mybir.dt.bfloat16
AF = mybir.ActivationFunctionType
ALU = mybir.AluOpType

D = 1024
L = 256
S = 2048
TOK = L + S
NT = TOK // 128
DEPTH = 4
NCORES = 8
BPC = 2
GROUPS = [(0, 256), (256, 512), (768, 512), (1280, 512), (1792, 512)]
EPS = 1e-6
INW = 7680
C_RQ, C_RK, C_RV, C_RG = 0, 512, 1024, 1536
C_WQ, C_WK, C_WV, C_WG = 2048, 2560, 2688, 2816
C_AQ, C_AK, C_AV, C_AG = 3328, 3840, 3968, 4096
C_MG = 4608

P_CT = 0
P_BMOD = 24
P_NORMW = 120
P_FNW = 152
P_QG = 160
P_KG = 164
P_SINK = 168
P_DEC = 200
P_EPS = 232
P_PEXP = 233
NPAR = 257


import os as _os
NOWAR = bool(int(_os.environ.get("NOWAR", "0")))


class _Res:
    __slots__ = ("w", "r")

    def __init__(self):
        self.w = None
        self.r = {}


class Builder:
    ND = 40
    QSLOTS = {"sp": list(range(0, 24)), "pool": list(range(24, 40)), "act": list(range(0, 24))}

    def __init__(self):
        self.nc = bass.Bass("TRN2", target_bir_lowering=False)
        nc = self.nc
        self.es = contextlib.ExitStack()
        self.eng = {"pe": nc.tensor, "dve": nc.vector, "act": nc.scalar, "pool": nc.gpsimd, "sp": nc.sync}
        self.sem = {}
        self.cnt = {}
        for e in self.eng:
            self.sem[e] = self.es.enter_context(nc.semaphore("sem_" + e))
            self.cnt[e] = 0
        for k in range(self.ND):
            self.sem[("d", k)] = self.es.enter_context(nc.semaphore("dsem%d" % k))
            self.cnt[("d", k)] = 0
        self.seen = {e: {} for e in self.eng}
        self.res = {}
        self.dslot = {"sp": 0, "pool": 0, "act": 0}
        self.n_inst = 0
        self.uid = 0

    def sb(self, name, shape, dt, stack=None):
        self.uid += 1
        return (stack or self.es).enter_context(self.nc.sbuf_tensor("%s_%d" % (name, self.uid), list(shape), dt))

    def ps(self, name, shape, dt):
        return self.es.enter_context(self.nc.psum_tensor(name, list(shape), dt))

    def dram(self, name, shape, dt, kind):
        return self.nc.dram_tensor(name, list(shape), dt, kind=kind).ap()

    def _deps(self, e, reads, writes):
        deps = {}
        res = self.res
        for k in reads:
            R = res.get(k)
            if R is None:
                R = res[k] = _Res()
            if R.w is not None:
                f, n = R.w
                if not (f == e and e == "pe"):
                    if deps.get(f, 0) < n:
                        deps[f] = n
        for k in writes:
            R = res.get(k)
            if R is None:
                R = res[k] = _Res()
            if R.w is not None:
                f, n = R.w
                if not (f == e and e == "pe"):
                    if deps.get(f, 0) < n:
                        deps[f] = n
            for f, n in R.r.items():
                if f == e and (e == "pe" or NOWAR):
                    continue
                if deps.get(f, 0) < n:
                    deps[f] = n
        return deps

    def _wait(self, e, deps):
        seen = self.seen[e]
        eng = self.eng[e]
        for f, n in deps.items():
            if seen.get(f, 0) < n:
                eng.wait_ge(self.sem[f], n)
                seen[f] = n

    def _record(self, who, n, reads, writes):
        res = self.res
        for k in reads:
            res[k].r[who] = n
        for k in writes:
            R = res[k]
            R.w = (who, n)
            R.r = {}

    def op(self, e, emit, r=(), w=()):
        deps = self._deps(e, r, w)
        self._wait(e, deps)
        inst = emit()
        self.cnt[e] += 1
        n = self.cnt[e]
        inst.then_inc(self.sem[e], 1)
        self._record(e, n, r, w)
        self.n_inst += 1
        return inst

    def dma(self, q, out, in_, r=(), w=()):
        deps = self._deps(q, r, w)
        sl = self.QSLOTS[q]
        k = sl[self.dslot[q] % len(sl)]
        self.dslot[q] += 1
        who = ("d", k)
        if self.cnt[who] > 0:
            deps[who] = max(deps.get(who, 0), self.cnt[who])
        self._wait(q, deps)
        inst = self.eng[q].dma_start(out=out, in_=in_)
        self.cnt[who] += 16
        inst.then_inc(self.sem[who], 16)
        self._record(who, self.cnt[who], r, w)
        self.n_inst += 1
        return inst

    def barrier(self, final=False):
        for e in self.eng:
            deps = {f: n for f, n in self.cnt.items() if n > 0 and f != e
                    and not (final is False and isinstance(f, tuple) and f[1] in self.QSLOTS["pool"])}
            self._wait(e, deps)

    def mm(self, out, lhsT, rhs, start, stop, r, w):
        return self.op("pe", lambda: self.nc.tensor.matmul(out, lhsT=lhsT, rhs=rhs, start=start, stop=stop), r, w)

    def act(self, out, in_, func, r, w, bias=None, scale=None, accum_out=None):
        kw = {}
        if bias is not None:
            kw["bias"] = bias
        if scale is not None:
            kw["scale"] = scale
        if accum_out is not None:
            kw["accum_out"] = accum_out
        return self.op("act", lambda: self.nc.scalar.activation(out=out, in_=in_, func=func, **kw), r, w)

    def tt(self, e, out, in0, in1, op, r, w):
        return self.op(e, lambda: self.eng[e].tensor_tensor(out=out, in0=in0, in1=in1, op=op), r, w)

    def ts(self, e, out, in0, s1, s2, op0, op1, r, w):
        if s2 is None:
            return self.op(e, lambda: self.eng[e].tensor_scalar(out=out, in0=in0, scalar1=s1, scalar2=None, op0=op0), r, w)
        return self.op(e, lambda: self.eng[e].tensor_scalar(out=out, in0=in0, scalar1=s1, scalar2=s2, op0=op0, op1=op1), r, w)

    def stt(self, e, out, in0, scalar, in1, op0, op1, r, w):
        return self.op(e, lambda: self.eng[e].scalar_tensor_tensor(out=out, in0=in0, scalar=scalar, in1=in1, op0=op0, op1=op1), r, w)

    def cp(self, e, out, in_, r, w):
        if e == "act":
            return self.op(e, lambda: self.nc.scalar.copy(out=out, in_=in_), r, w)
        return self.op(e, lambda: self.eng[e].tensor_copy(out=out, in_=in_), r, w)

    def recip(self, out, in_, r, w):
        return self.op("dve", lambda: self.nc.vector.reciprocal(out=out, in_=in_), r, w)

    def memset(self, e, ap, val, w):
        return self.op(e, lambda: self.eng[e].memset(ap, val), (), w)


class Ring:
    def __init__(self, b, name, n, shape, dt, stack=None):
        self.bufs = []
        for i in range(n):
            t = b.sb(name, shape, dt, stack)
            b.uid += 1
            self.bufs.append((t, "%s#%d" % (name, b.uid)))
        self.i = 0

    def next(self):
        t = self.bufs[self.i]
        self.i = (self.i + 1) % len(self.bufs)
        return t


class Prog(Builder):
    def __init__(self, n_layers=DEPTH, n_batch=BPC, stage=99, dbg=False, branches=("ax", "win", "ret")):
        super().__init__()
        self.n_layers = n_layers
        self.n_batch = n_batch
        self.stage = stage
        self.branches = branches
        self.dbg = dbg
        self.marks = []
        self.build()

    def mark(self, name):
        self.marks.append((name, dict((k, v) for k, v in self.cnt.items() if not isinstance(k, tuple))))

    def build(self):
        b = self
        self.xin = b.dram("xin", [BPC, TOK, D], F32, "ExternalInput")
        self.par_d = b.dram("par", [128, NPAR], F32, "ExternalInput")
        self.cst_d = b.dram("cst", [128, 4, 128], F32, "ExternalInput")
        self.axtab_d = b.dram("axtab", [128, 2, S], F32, "ExternalInput")
        self.rtab_d = b.dram("rtab", [128, 2, 20, 64], F32, "ExternalInput")
        self.wmask_d = b.dram("wmask", [128, 384], F32, "ExternalInput")
        self.dtab_d = b.dram("dtab", [128, 4, 128], F32, "ExternalInput")
        self.w_in = b.dram("w_in", [DEPTH, D, INW], F32, "ExternalInput")
        self.w_kd = b.dram("w_kd", [DEPTH, D, 512], F32, "ExternalInput")
        self.w_mod = b.dram("w_mod", [DEPTH, D, 3 * D], F32, "ExternalInput")
        self.w_p = b.dram("w_p", [DEPTH, 3, 512, D], F32, "ExternalInput")
        self.w_out = b.dram("w_out", [DEPTH, D, D], F32, "ExternalInput")
        self.y = b.dram("y", [BPC, S, D], F32, "ExternalOutput")
        self.og_scr = b.dram("og_scr", [12, 128, TOK], BF16, "ExternalOutput" if self.dbg else "Internal")
        self.s_attn = b.dram("s_attn", [DEPTH, 2, 128, 8, 1408], BF16, "Internal")
        self.s_ret = b.dram("s_ret", [DEPTH, 4, 128, 8, 512], BF16, "Internal")
        self.s_mrg = b.dram("s_mrg", [DEPTH, 8, 128, 4608], BF16, "Internal")
        self.s_out = b.dram("s_out", [DEPTH, 128, 8, 1024], BF16, "Internal")
        if self.dbg:
            self.hT_d = b.dram("hT_d", [128, 8, TOK], BF16, "ExternalOutput")

        self.xT = b.sb("xT", [128, 8, TOK], F32)
        self.hT = b.sb("hT", [128, 8, TOK], BF16)
        self.par = b.sb("par", [128, NPAR], F32)
        self.cst = b.sb("cst", [128, 4, 128], F32)
        self.ident_bf = b.sb("identbf", [128, 128], BF16)
        self.wmask = b.sb("wmask", [128, 384], BF16)
        self.dtab = b.sb("dtab", [128, 4, 128], F32)
        self.modv = b.sb("modv", [128, DEPTH, 3, 3, 8], F32)
        self.lg = b.sb("lg", [128, 32], F32)
        self.esink = b.sb("esink", [128, 32], F32)
        self.Dm = b.sb("Dm", [128, 8, 128], BF16)
        self.decs = b.sb("decs", [128, 3, 8], F32)
        self.pb = [b.ps("pb%d" % i, [128, 512], F32) for i in range(8)]
        self.ident_f = self.cst[:, 0, :]
        self.swap_f = self.cst[:, 1, :]
        self.ones_f = self.cst[:, 2, :]
        self.bones_f = self.cst[:, 3, :]
        self.eps_ap = self.par[:, P_EPS:P_EPS + 1]

        self.convert(0)
        self.prologue()
        for bi in range(self.n_batch):
            self.load_x(bi)
            for l in range(self.n_layers):
                if bi == 0 and l + 1 < self.n_layers:
                    self.convert(l + 1)
                self.layer(bi, l)
            if self.stage >= 99:
                self.final(bi)
            else:
                self.dump_xT(bi)
        self.barrier(final=True)

    def pcol(self, c, n=1):
        return self.par[:, c:c + n]

    def convert(self, l):
        b = self

        if not hasattr(self, "ckeys"):
            self.ckeys = {}

        def cv(dst, src, key):
            lst = self.ckeys.setdefault(key, [])
            k2 = key + (len(lst),)
            lst.append(k2)
            b.dma("pool", dst, src.rearrange("(kc p) n -> p kc n", p=128), w=[k2])

        w = self.w_in[l]
        for ki, (cq, cg, cv_, kd0) in enumerate(((C_WQ, C_WG, C_WV, 0), (C_AQ, C_AG, C_AV, 256))):
            key = ("s_attn", l, ki)
            dst = self.s_attn[l, ki]
            cv(dst[:, :, 0:512], w[:, cq:cq + 512], key)
            cv(dst[:, :, 512:1024], w[:, cg:cg + 512], key)
            cv(dst[:, :, 1024:1280], self.w_kd[l][:, kd0:kd0 + 256], key)
            cv(dst[:, :, 1280:1408], w[:, cv_:cv_ + 128], key)
        for h in range(4):
            key = ("s_ret", l, h)
            dst = self.s_ret[l, h]
            for i, c0 in enumerate((C_RQ, C_RK, C_RV, C_RG)):
                cv(dst[:, :, i * 128:(i + 1) * 128], w[:, c0 + h * 128:c0 + (h + 1) * 128], key)
        for dc in range(8):
            key = ("s_mrg", l, dc)
            dmg = self.s_mrg[l, dc][:, 0:3072].rearrange("p (kc b n) -> p kc b n", kc=8, b=3)
            dwp = self.s_mrg[l, dc][:, 3072:4608].rearrange("p (b kc n) -> p b kc n", b=3, kc=4)
            for br in range(3):
                c0 = C_MG + br * 1024 + dc * 128
                cv(dmg[:, :, br, :], w[:, c0:c0 + 128], key)
                cv(dwp[:, br], self.w_p[l, br][:, dc * 128:(dc + 1) * 128], key)
        cv(self.s_out[l], self.w_out[l], ("s_out", l))

    def prologue(self):
        b = self
        b.dma("sp", self.par[:], self.par_d, w=["par"])
        b.dma("sp", self.cst[:], self.cst_d, w=["cst"])
        b.dma("sp", self.dtab[:], self.dtab_d, w=["dtab"])
        with contextlib.ExitStack() as st:
            wmf = b.sb("wmf", [128, 384], F32, st)
            b.dma("sp", wmf[:], self.wmask_d, w=["wmf"])
            b.cp("dve", self.wmask[:], wmf[:], r=["wmf"], w=["wmask"])
            b.cp("dve", self.ident_bf[:], self.ident_f, r=["cst"], w=["identbf"])
            t1 = b.sb("plt", [128, 32], F32, st)
            b.act(t1[:], self.pcol(P_DEC, 32), AF.Exp, r=["par"], w=["plt"], scale=-1.0)
            b.act(t1[:], t1[:], AF.Ln, r=["plt"], w=["plt"], bias=1.0)
            b.ts("dve", self.lg[:], t1[:], -1.0, None, ALU.mult, None, r=["plt"], w=["lg"])
            b.act(self.esink[:], self.pcol(P_SINK, 32), AF.Exp, r=["par"], w=["esink"])
            sc = b.sb("sc", [128, 24], F32, st)
            b.act(sc[:], self.pcol(P_CT, 24), AF.Silu, r=["par"], w=["sc"])
            wring = Ring(b, "wmodc", 2, [128, 3 * D], F32, st)
            modraw = b.sb("modraw", [128, 24, 3], F32, st)
            for l in range(self.n_layers):
                bank = self.pb[l % 2]
                bk = "pb%d" % (l % 2)
                for kc in range(8):
                    wt, wk = wring.next()
                    b.dma("sp", wt[:], self.w_mod[l, kc * 128:(kc + 1) * 128, :], w=[wk])
                    for ch in range(24):
                        b.mm(bank[:, ch * 3:ch * 3 + 3], lhsT=wt[:, ch * 128:(ch + 1) * 128],
                             rhs=sc[:, kc * 3:kc * 3 + 3], start=(kc == 0 and ch == 0), stop=(kc == 7 and ch == 23),
                             r=[wk, "sc"], w=[bk])
                bm = self.par[:, P_BMOD + l * 24:P_BMOD + (l + 1) * 24]
                b.tt("dve", modraw[:], bank[:, 0:72].rearrange("p (c j) -> p c j", j=3),
                     bm.unsqueeze(2).to_broadcast([128, 24, 3]), ALU.add, r=[bk, "par"], w=["modraw"])
                nw = self.par[:, P_NORMW + l * 8:P_NORMW + (l + 1) * 8]
                for j in range(3):
                    b.stt("dve", self.modv[:, l, j, 0, :], modraw[:, 8:16, j], 1.0, nw, ALU.add, ALU.mult,
                          r=["modraw", "par"], w=["modv"])
                    b.cp("dve", self.modv[:, l, j, 1, :], modraw[:, 0:8, j], r=["modraw"], w=["modv"])
                    b.cp("dve", self.modv[:, l, j, 2, :], modraw[:, 16:24, j], r=["modraw"], w=["modv"])
            b.barrier()

    def load_x(self, bi):
        b = self
        with contextlib.ExitStack() as st:
            xs_ring = Ring(b, "xs", 2, [128, D], F32, st)
            for t in range(NT):
                xs, xk = xs_ring.next()
                b.dma("sp", xs[:], self.xin[bi, t * 128:(t + 1) * 128, :], w=[xk])
                gi = self.gi_of_tile(t)
                for half in range(2):
                    bank = self.pb[(2 * t + half) % 4]
                    bk = "pb%d" % ((2 * t + half) % 4)
                    for q in range(4):
                        kc = half * 4 + q
                        b.mm(bank[:, q * 128:(q + 1) * 128], lhsT=xs[:, kc * 128:(kc + 1) * 128], rhs=self.ident_f,
                             start=True, stop=True, r=[xk, "cst"], w=[bk])
                    e = "act" if half == 0 else "dve"
                    b.cp(e, self.xT[:, half * 4:(half + 1) * 4, t * 128:(t + 1) * 128],
                         bank[:, :].rearrange("p (q n) -> p q n", q=4), r=[bk], w=[("xT", gi)])
            b.barrier()

    @staticmethod
    def gi_of_tile(t):
        return 0 if t < 2 else 1 + (t - 2) // 4

    def rstd_from_psum(self, out_ap, okey, in_ap, ikey, inv_n):
        b = self
        b.act(out_ap, in_ap, AF.Ln, r=[ikey, "par"], w=[okey], scale=inv_n, bias=self.eps_ap)
        b.act(out_ap, out_ap, AF.Exp, r=[okey], w=[okey], scale=-0.5)

    def rms_rstd(self, st_rings, src, g0, N, gi, inv_n, bank, bk):
        b = self
        sq_ring, rs_ring = st_rings
        for kc in range(8):
            sq, sk = sq_ring.next()
            b.act(sq[:, :N], src[:, kc, g0:g0 + N], AF.Square, r=[("xT", gi)], w=[sk])
            b.mm(bank[:, :N], lhsT=self.ones_f, rhs=sq[:, :N], start=(kc == 0), stop=(kc == 7), r=[sk, "cst"], w=[bk])
        rs, rk = rs_ring.next()
        self.rstd_from_psum(rs[:, :N], rk, bank[:, :N], bk, inv_n)
        return rs, rk

    def final(self, bi):
        b = self
        with contextlib.ExitStack() as st:
            rings = (Ring(b, "fsq", 3, [128, 512], F32, st), Ring(b, "frs", 2, [128, 512], F32, st))
            yT = b.sb("yT", [128, 8, 512], F32, st)
            ys_ring = Ring(b, "ys", 2, [128, D], F32, st)
            for gi in range(1, 5):
                g0, N = GROUPS[gi]
                rs, rk = self.rms_rstd(rings, self.xT, g0, N, gi, 1.0 / D, self.pb[gi % 2], "pb%d" % (gi % 2))
                for kc in range(8):
                    b.stt("dve", yT[:, kc, :], self.xT[:, kc, g0:g0 + N], self.pcol(P_FNW + kc), rs[:, :N],
                          ALU.mult, ALU.mult, r=[("xT", gi), rk, "par"], w=[("yT", kc)])
                for tt in range(4):
                    ys, yk = ys_ring.next()
                    for half in range(2):
                        bank = self.pb[2 + (2 * tt + half) % 4]
                        bk = "pb%d" % (2 + (2 * tt + half) % 4)
                        for q in range(4):
                            kc = half * 4 + q
                            b.mm(bank[:, q * 128:(q + 1) * 128], lhsT=yT[:, kc, tt * 128:(tt + 1) * 128],
                                 rhs=self.ident_f, start=True, stop=True, r=[("yT", kc), "cst"], w=[bk])
                        e = "act" if half == 0 else "dve"
                        b.cp(e, ys[:, half * 512:(half + 1) * 512], bank[:, :], r=[bk], w=[yk])
                    p0 = g0 - L + tt * 128
                    b.dma("sp", self.y[bi, p0:p0 + 128, :], ys[:], r=[yk], w=["y_out"])
            b.barrier()

    def dump_xT(self, bi):
        b = self
        with contextlib.ExitStack() as st:
            ys_ring = Ring(b, "ys", 2, [128, D], F32, st)
            for t in range(2, NT):
                ys, yk = ys_ring.next()
                gi = self.gi_of_tile(t)
                for half in range(2):
                    bank = self.pb[2 + (2 * t + half) % 4]
                    bk = "pb%d" % (2 + (2 * t + half) % 4)
                    for q in range(4):
                        kc = half * 4 + q
                        b.mm(bank[:, q * 128:(q + 1) * 128], lhsT=self.xT[:, kc, t * 128:(t + 1) * 128],
                             rhs=self.ident_f, start=True, stop=True, r=[("xT", gi), "cst"], w=[bk])
                    b.cp("act" if half == 0 else "dve", ys[:, half * 512:(half + 1) * 512], bank[:, :], r=[bk], w=[yk])
                b.dma("sp", self.y[bi, (t - 2) * 128:(t - 1) * 128, :], ys[:], r=[yk], w=["y_out"])
            b.barrier()

    def layer(self, bi, l):
        self.mark("L%d.%d tables" % (bi, l))
        self.layer_tables(l)
        self.mark("L%d.%d norm" % (bi, l))
        self.phase_norm(bi, l)
        for name, br in (("ax", 2), ("win", 1), ("ret", 0)):
            if name not in self.branches:
                self.zero_og(br)
        if "ax" in self.branches:
            self.mark("L%d.%d ax" % (bi, l))
            self.phase_attn(bi, l, "ax")
        if "win" in self.branches:
            self.mark("L%d.%d win" % (bi, l))
            self.phase_attn(bi, l, "win")
        if "ret" in self.branches:
            self.mark("L%d.%d ret" % (bi, l))
            self.phase_ret(bi, l)
        self.mark("L%d.%d merge" % (bi, l))
        self.phase_merge(bi, l)
        self.mark("L%d.%d end" % (bi, l))

    def zero_og(self, br):
        b = self
        with contextlib.ExitStack() as st:
            z = b.sb("zog", [128, TOK], BF16, st)
            b.memset("dve", z[:], 0.0, w=["zog"])
            for c in range(4):
                for gi, (g0, N) in enumerate(GROUPS):
                    b.dma("sp", self.og_scr[br * 4 + c][:, g0:g0 + N], z[:, g0:g0 + N], r=["zog"], w=[("ogs", br * 4 + c, gi)])
            b.barrier()

    def layer_tables(self, l):
        b = self
        with contextlib.ExitStack() as st:
            tmp = b.sb("ltmp", [128, 128], F32, st)
            t8 = b.sb("lt8", [128, 3, 8], F32, st)
            lg8 = self.lg[:, l * 8:(l + 1) * 8]
            for row in range(3):
                b.tt("dve", t8[:, row, :], self.par[:, P_PEXP + row * 8:P_PEXP + (row + 1) * 8], lg8, ALU.mult,
                     r=["par", "lg"], w=["lt8"])
            b.act(self.decs[:].rearrange("p a b -> p (a b)"), t8[:].rearrange("p a b -> p (a b)"), AF.Exp,
                  r=["lt8"], w=["decs"])
            b.ts("dve", self.decs[:, 1, :], self.decs[:, 1, :], float(128 ** -0.5), None, ALU.mult, None,
                 r=["decs"], w=["decs"])
            for d in range(2):
                for h in range(4):
                    i = d * 4 + h
                    b.act(tmp[:], self.dtab[:, d, :], AF.Exp, r=["dtab", "lg"], w=["ltmp"], scale=self.lg[:, l * 8 + i:l * 8 + i + 1])
                    b.tt("dve", self.Dm[:, i, :], tmp[:], self.dtab[:, 2 + d, :], ALU.mult, r=["ltmp", "dtab"], w=["Dm"])
            b.barrier()

    def phase_norm(self, bi, l):
        b = self
        with contextlib.ExitStack() as st:
            rings = (Ring(b, "nsq", 3, [128, 512], F32, st), Ring(b, "nrs", 2, [128, 512], F32, st))
            tmp_ring = Ring(b, "ntmp", 3, [128, 512], F32, st)
            for gi, (g0, N) in enumerate(GROUPS):
                j = 2 if gi == 0 else bi
                rs, rk = self.rms_rstd(rings, self.xT, g0, N, gi, 1.0 / D, self.pb[gi % 2], "pb%d" % (gi % 2))
                for kc in range(8):
                    tp, tk = tmp_ring.next()
                    b.stt("dve", tp[:, :N], self.xT[:, kc, g0:g0 + N], self.modv[:, l, j, 0, kc:kc + 1], rs[:, :N],
                          ALU.mult, ALU.mult, r=[("xT", gi), rk, "modv"], w=[tk])
                    b.act(self.hT[:, kc, g0:g0 + N], tp[:, :N], AF.Identity, r=[tk, "modv"], w=[("hT", gi)],
                          bias=self.modv[:, l, j, 1, kc:kc + 1])
            b.barrier()
            if self.dbg:
                b.dma("sp", self.hT_d, self.hT[:], r=[("hT", g) for g in range(5)], w=["hT_d"])

    def phase_attn(self, bi, l, kind):
        b = self
        ax = (kind == "ax")
        ki = 1 if ax else 0
        br = 2 if ax else 1
        scale = 0.125
        LOOK2 = 1
        with contextlib.ExitStack() as st:
            wt = b.sb("wqkv", [128, 8, 1408], BF16, st)
            tab = b.sb("axtab", [128, 2, S], F32, st)
            kT = b.sb("kT", [128, 2, TOK], BF16, st)
            Va = b.sb("Va", [128, NT, 2, 128], BF16, st)
            tf = Ring(b, "tf", 3, [128, 512], F32, st)
            tg = Ring(b, "tg", 4, [128, 512], F32, st)
            qT_ring = Ring(b, "qT", 6, [128, 512], BF16, st)
            sg_ring = Ring(b, "sg", 6, [128, 512], BF16, st)
            P_ring = Ring(b, "P", 4, [128, 512], BF16, st)
            og_ring = Ring(b, "og", 2, [128, 512], BF16, st)
            pbP = [(self.pb[0], "pb0"), (self.pb[1], "pb1")]
            pbM = (self.pb[2], "pb2")
            pbS = [(self.pb[3], "pb3"), (self.pb[4], "pb4"), (self.pb[5], "pb5"), (self.pb[2], "pb2")]
            pbO = [(self.pb[6], "pb6"), (self.pb[7], "pb7")]
            cnt = {"p": 0, "s": 0, "o": 0}

            def nextP():
                cnt["p"] += 1
                return pbP[cnt["p"] % 2]

            def nextS():
                cnt["s"] += 1
                return pbS[cnt["s"] % 4]

            def nextO():
                cnt["o"] += 1
                return pbO[cnt["o"] % 2]

            b.dma("sp", wt[:], self.s_attn[l, ki], r=self.ckeys[("s_attn", l, ki)], w=["wqkv"])
            b.dma("sp", tab[:], self.axtab_d, w=["axtab"])
            b.memset("dve", Va[:, :, :, 64:128], 1.0, w=["Va1"])
            WQ, WG, WK, WV = 0, 512, 1024, 1280

            def rope(src_ap, skey, g0, N, out_ap, okey):
                p0 = g0 - L
                bank, bk = pbM
                b.mm(bank[:, :N], lhsT=self.swap_f, rhs=src_ap, start=True, stop=True, r=[skey, "cst"], w=[bk])
                t1, k1 = tf.next()
                b.tt("dve", t1[:, :N], src_ap, tab[:, 0, p0:p0 + N], ALU.mult, r=[skey, "axtab"], w=[k1])
                t2, k2 = tf.next()
                b.tt("dve", t2[:, :N], bank[:, :N], tab[:, 1, p0:p0 + N], ALU.mult, r=[bk, "axtab"], w=[k2])
                b.tt("dve", out_ap, t1[:, :N], t2[:, :N], ALU.add, r=[k1, k2], w=[okey])

            def headnorm(bank, bk, N, gain_col):
                sq, sk = tf.next()
                b.act(sq[:, :N], bank[:, :N], AF.Square, r=[bk], w=[sk])
                mb, mk = pbM
                b.mm(mb[:, :N], lhsT=self.bones_f, rhs=sq[:, :N], start=True, stop=True, r=[sk, "cst"], w=[mk])
                rs, rk = tg.next()
                self.rstd_from_psum(rs[:, :N], rk, mb[:, :N], mk, 1.0 / 64)
                qn, qk = tg.next()
                b.stt("dve", qn[:, :N], bank[:, :N], gain_col, rs[:, :N], ALU.mult, ALU.mult, r=[bk, rk, "par"], w=[qk])
                return qn, qk

            for gi, (g0, N) in enumerate(GROUPS):
                for kv in range(2):
                    bank, bk = nextP()
                    for kc in range(8):
                        b.mm(bank[:, :N], lhsT=wt[:, kc, WK + kv * 128:WK + (kv + 1) * 128], rhs=self.hT[:, kc, g0:g0 + N],
                             start=(kc == 0), stop=(kc == 7), r=["wqkv", ("hT", gi)], w=[bk])
                    okey = ("kT", kv, gi)
                    if ax:
                        kn, kk = headnorm(bank, bk, N, self.pcol(P_KG + l))
                    else:
                        kn, kk = tg.next()
                        b.cp("act", kn[:, :N], bank[:, :N], r=[bk], w=[kk])
                    if gi == 0:
                        b.cp("dve", kT[:, kv, g0:g0 + N], kn[:, :N], r=[kk], w=[okey])
                    else:
                        rope(kn[:, :N], kk, g0, N, kT[:, kv, g0:g0 + N], okey)
            for t in range(NT):
                bank, bk = nextP()
                gi = self.gi_of_tile(t)
                for kc in range(8):
                    b.mm(bank[:, 0:128], lhsT=self.hT[:, kc, t * 128:(t + 1) * 128], rhs=wt[:, kc, WV:WV + 128],
                         start=(kc == 0), stop=(kc == 7), r=["wqkv", ("hT", gi)], w=[bk])
                b.cp("act" if t % 2 == 0 else "dve", Va[:, t, :, 0:64], bank[:, 0:128].rearrange("p (g d) -> p g d", g=2),
                     r=[bk], w=[("Va", t)])

            for gi, (g0, N) in enumerate(GROUPS):
                if gi == 0:
                    items = [(0, 0, N, None), (1, 0, N, None)]
                elif ax:
                    items = [(t, 0, N, None) for t in range(NT)]
                else:
                    items = [(0, 0, N, None), (1, 0, N, None)]
                    i0 = 4 * (gi - 1)
                    for kb in range(i0 - 1, i0 + 5):
                        if kb < 0 or kb > 15:
                            continue
                        qlo = max(kb - 1, i0)
                        qhi = min(kb + 1, i0 + 3)
                        items.append((2 + kb, (qlo - i0) * 128, (qhi - i0 + 1) * 128, (qlo - kb + 1) * 128))
                n_it = len(items)
                qTs, sgs = [], []
                for qc in range(4):
                    bank, bk = nextP()
                    for kc in range(8):
                        b.mm(bank[:, :N], lhsT=wt[:, kc, WQ + qc * 128:WQ + (qc + 1) * 128], rhs=self.hT[:, kc, g0:g0 + N],
                             start=(kc == 0), stop=(kc == 7), r=["wqkv", ("hT", gi)], w=[bk])
                    if ax:
                        qn, qk = headnorm(bank, bk, N, self.pcol(P_QG + l))
                    else:
                        qn, qk = tg.next()
                        b.cp("act", qn[:, :N], bank[:, :N], r=[bk], w=[qk])
                    qT, qTk = qT_ring.next()
                    if gi == 0:
                        b.cp("dve", qT[:, :N], qn[:, :N], r=[qk], w=[qTk])
                    else:
                        rope(qn[:, :N], qk, g0, N, qT[:, :N], qTk)
                    qTs.append((qT, qTk))
                    bank, bk = nextP()
                    for kc in range(8):
                        b.mm(bank[:, :N], lhsT=wt[:, kc, WG + qc * 128:WG + (qc + 1) * 128], rhs=self.hT[:, kc, g0:g0 + N],
                             start=(kc == 0), stop=(kc == 7), r=["wqkv", ("hT", gi)], w=[bk])
                    e1, ek = tg.next()
                    b.act(e1[:, :N], bank[:, :N], AF.Exp, r=[bk], w=[ek], scale=-1.0)
                    b.act(e1[:, :N], e1[:, :N], AF.Ln, r=[ek], w=[ek], bias=1.0)
                    b.act(e1[:, :N], e1[:, :N], AF.Exp, r=[ek], w=[ek], scale=-1.0)
                    sg, sgk = sg_ring.next()
                    b.tt("dve", sg[:, :N], bank[:, :N], e1[:, :N], ALU.mult, r=[bk, ek], w=[sgk])
                    sgs.append((sg, sgk))
                for qc in range(4):
                    kv = qc // 2
                    qT, qTk = qTs[qc]
                    sg, sgk = sgs[qc]
                    og, ogk = og_ring.next()
                    obs = [nextO(), nextO()]
                    pend = []

                    def tail(pds, idx):
                        for hh, (sb_, sbk_, t_, c0_, c1_, m0_) in enumerate(pds):
                            ob, obk = obs[hh]
                            n_ = c1_ - c0_
                            Pt, Pk = P_ring.next()
                            b.act(Pt[:, :n_], sb_[:, :n_], AF.Exp, r=[sbk_], w=[Pk], scale=scale)
                            if m0_ is not None:
                                b.tt("dve", Pt[:, :n_], Pt[:, :n_], self.wmask[:, m0_:m0_ + n_], ALU.mult,
                                     r=[Pk, "wmask"], w=[Pk])
                            b.mm(ob[:, c0_:c1_], lhsT=Va[:, t_, kv, :], rhs=Pt[:, :n_], start=(idx == 0), stop=(idx == n_it - 1),
                                 r=[("Va", t_), "Va1", Pk], w=[obk])

                    for idx, (t, c0, c1, m0) in enumerate(items):
                        n = c1 - c0
                        ents = []
                        for hh in range(2):
                            ph = hh * 64
                            sbank, sbk = nextS()
                            b.mm(sbank[:, :n], lhsT=kT[ph:ph + 64, kv, t * 128:(t + 1) * 128], rhs=qT[ph:ph + 64, c0:c1],
                                 start=True, stop=True, r=[("kT", kv, self.gi_of_tile(t)), qTk], w=[sbk])
                            ents.append((sbank, sbk, t, c0, c1, m0))
                        pend.append(ents)
                        if len(pend) > LOOK2:
                            tail(pend.pop(0), idx - LOOK2)
                    base = n_it - len(pend)
                    for i_, pd in enumerate(pend):
                        tail(pd, base + i_)
                    for hh in range(2):
                        head = 2 * qc + hh
                        ph = hh * 64
                        ob, obk = obs[hh]
                        rec, rck = tg.next()
                        if ax:
                            b.act(rec[64:128, :N], ob[64:128, :N], AF.Ln, r=[obk], w=[rck])
                        else:
                            b.act(rec[64:128, :N], ob[64:128, :N], AF.Ln, r=[obk, "esink"], w=[rck],
                                  bias=self.esink[64:128, l * 8 + head:l * 8 + head + 1])
                        b.act(rec[64:128, :N], rec[64:128, :N], AF.Exp, r=[rck], w=[rck], scale=-1.0)
                        u, uk = tf.next()
                        b.tt("dve", u[64:128, :N], ob[0:64, :N], sg[ph:ph + 64, :N], ALU.mult, r=[obk, sgk], w=[uk])
                        b.tt("dve", og[ph:ph + 64, :N], u[64:128, :N], rec[64:128, :N], ALU.mult, r=[uk, rck], w=[ogk])
                    b.dma("sp", self.og_scr[br * 4 + qc][:, g0:g0 + N], og[:, :N], r=[ogk], w=[("ogs", br * 4 + qc, gi)])
            b.barrier()

    def phase_ret(self, bi, l):
        b = self
        NSK = 6
        with contextlib.ExitStack() as st:
            rtab = b.sb("rtab", [128, 2, 20, 64], F32, st)
            b.dma("sp", rtab[:], self.rtab_d, w=["rtab"])
            wr = b.sb("wr", [128, 8, 512], BF16, st)
            out_f = b.sb("outf", [128, NT, 128], F32, st)
            sgT = b.sb("sgT", [128, TOK], BF16, st)
            ogT = b.sb("ogT", [128, TOK], BF16, st)
            R = b.sb("R", [128, 128], F32, st)
            Rb = b.sb("Rb", [128, 128], BF16, st)
            xqk_r = Ring(b, "xqk", NSK, [128, 2, 64, 2], F32, st)
            ra = Ring(b, "ra", NSK, [128, 4, 2, 64], F32, st)
            qkr_r = Ring(b, "qkr", NSK, [128, 2, 64, 2], BF16, st)
            v_r = Ring(b, "vsb", NSK, [128, 128], BF16, st)
            kd_r = Ring(b, "kd", NSK, [128, 128], BF16, st)
            qkT_r = Ring(b, "qkT", NSK, [128, 256], BF16, st)
            A_r = Ring(b, "A", NSK, [128, 128], BF16, st)
            ti_r = Ring(b, "ti", NSK, [128, 128], F32, st)
            tot_r = Ring(b, "tot", NSK, [128, 128], F32, st)
            on_r = Ring(b, "on", NSK, [128, 128], BF16, st)
            ssq_r = Ring(b, "ssq", NSK, [128, 2], F32, st)
            junk_r = Ring(b, "junk", 2, [128, 128], F32, st)
            gt_r = Ring(b, "gt", 2, [128, 512], F32, st)
            kdb = b.sb("kdb", [128, 8, 128], F32, st)
            for i8 in range(8):
                b.ts("dve", kdb[:, i8, :], self.ones_f, self.decs[:, 1, i8:i8 + 1], None, ALU.mult, None,
                     r=["cst", "decs"], w=["kdb"])
            import os
            for h in range(int(os.environ.get("RET_H", "4"))):
                b.dma("sp", wr[:], self.s_ret[l, h], r=self.ckeys[("s_ret", l, h)], w=["wr"])
                for gi, (g0, N) in enumerate(GROUPS):
                    bank, bk = self.pb[gi % 2], "pb%d" % (gi % 2)
                    for kc in range(8):
                        b.mm(bank[:, :N], lhsT=wr[:, kc, 384:512], rhs=self.hT[:, kc, g0:g0 + N],
                             start=(kc == 0), stop=(kc == 7), r=["wr", ("hT", gi)], w=[bk])
                    e1, ek = gt_r.next()
                    b.act(e1[:, :N], bank[:, :N], AF.Exp, r=[bk], w=[ek], scale=-1.0)
                    b.act(e1[:, :N], e1[:, :N], AF.Ln, r=[ek], w=[ek], bias=1.0)
                    b.act(e1[:, :N], e1[:, :N], AF.Exp, r=[ek], w=[ek], scale=-1.0)
                    b.tt("dve", sgT[:, g0:g0 + N], bank[:, :N], e1[:, :N], ALU.mult, r=[bk, ek], w=[("sgT", gi)])
                for d in range(int(os.environ.get("RET_D", "2"))):
                    di = d * 4 + h
                    qdec = self.decs[:, 0, di:di + 1]
                    kdec = self.decs[:, 1, di:di + 1]
                    g128 = self.decs[:, 2, di:di + 1]
                    order = list(range(NT)) if d == 0 else [1, 0] + list(range(NT - 1, 1, -1))
                    b.memset("dve", R[:], 0.0, w=["R"])
                    b.memset("dve", Rb[:], 0.0, w=["Rb"])
                    stt_ = {}

                    def stage0(i):
                        t = order[i]
                        gi = self.gi_of_tile(t)
                        bank, bk = self.pb[i % 2], "pb%d" % (i % 2)
                        for kc in range(8):
                            b.mm(bank[:, 0:384], lhsT=self.hT[:, kc, t * 128:(t + 1) * 128], rhs=wr[:, kc, 0:384],
                                 start=(kc == 0), stop=(kc == 7), r=["wr", ("hT", gi)], w=[bk])
                        xq, xk = xqk_r.next()
                        b.cp("act", xq[:].rearrange("p a b c -> p (a b c)"), bank[:, 0:256], r=[bk], w=[xk])
                        vs, vk = v_r.next()
                        b.cp("act", vs[:], bank[:, 256:384], r=[bk], w=[vk])
                        tp = t if (d == 0 or t >= 2) else 18 + t
                        cosb = rtab[:, 0, tp:tp + 1, :].to_broadcast([128, 2, 64])
                        sinb = rtab[:, 1, tp:tp + 1, :].to_broadcast([128, 2, 64])
                        x0 = xq[:, :, :, 0]
                        x1 = xq[:, :, :, 1]
                        pr, pk = ra.next()
                        b.tt("dve", pr[:, 0], x0, cosb, ALU.mult, r=[xk, "rtab"], w=[(pk, 0)])
                        b.tt("dve", pr[:, 1], x1, sinb, ALU.mult, r=[xk, "rtab"], w=[(pk, 1)])
                        b.tt("dve", pr[:, 2], x0, sinb, ALU.mult, r=[xk, "rtab"], w=[(pk, 2)])
                        b.tt("dve", pr[:, 3], x1, cosb, ALU.mult, r=[xk, "rtab"], w=[(pk, 3)])
                        qr, qrk = qkr_r.next()
                        if d == 0:
                            b.tt("dve", qr[:, :, :, 0], pr[:, 0], pr[:, 1], ALU.subtract, r=[(pk, 0), (pk, 1)], w=[(qrk, 0)])
                            b.tt("dve", qr[:, :, :, 1], pr[:, 2], pr[:, 3], ALU.add, r=[(pk, 2), (pk, 3)], w=[(qrk, 1)])
                        else:
                            b.tt("dve", qr[:, :, :, 0], pr[:, 0], pr[:, 1], ALU.add, r=[(pk, 0), (pk, 1)], w=[(qrk, 0)])
                            b.tt("dve", qr[:, :, :, 1], pr[:, 3], pr[:, 2], ALU.subtract, r=[(pk, 2), (pk, 3)], w=[(qrk, 1)])
                        kd, kdk = kd_r.next()
                        b.tt("dve", kd[:], qr[:, 1].rearrange("p a b -> p (a b)"), kdb[:, di, :], ALU.mult,
                             r=[(qrk, 0), (qrk, 1), "kdb"], w=[kdk])
                        stt_[i] = dict(t=t, gi=gi, vs=vs, vk=vk, qr=qr, qrk=qrk, kd=kd, kdk=kdk)

                    def stage1(i):
                        s_ = stt_[i]
                        qr = s_["qr"]
                        bank, bk = self.pb[2 + i % 2], "pb%d" % (2 + i % 2)
                        for n_ in range(2):
                            b.mm(bank[:, n_ * 128:(n_ + 1) * 128], lhsT=qr[:, n_].rearrange("p a b -> p (a b)"), rhs=self.ident_bf[:],
                                 start=True, stop=True, r=[(s_["qrk"], 0), (s_["qrk"], 1), "identbf"], w=[bk])
                        qkT, qkTk = qkT_r.next()
                        b.cp("act", qkT[:], bank[:, 0:256], r=[bk], w=[qkTk])
                        s_["qkT"], s_["qkTk"] = qkT, qkTk

                    def stage2(i):
                        s_ = stt_[i]
                        qkT = s_["qkT"]
                        bank, bk = self.pb[4], "pb4"
                        c = (i % 2) * 128
                        b.mm(bank[:, c:c + 128], lhsT=qkT[:, 128:256], rhs=qkT[:, 0:128], start=True, stop=True,
                             r=[s_["qkTk"]], w=[bk])
                        A, Ak = A_r.next()
                        b.tt("dve", A[:], bank[:, c:c + 128], self.Dm[:, di, :], ALU.mult, r=[bk, "Dm"], w=[Ak])
                        s_["A"], s_["Ak"] = A, Ak

                    def stage3(i):
                        s_ = stt_[i]
                        t = s_["t"]
                        bank, bk = self.pb[5], "pb5"
                        c = (i % 2) * 256
                        b.mm(bank[:, c:c + 128], lhsT=s_["A"][:], rhs=s_["vs"][:], start=True, stop=True,
                             r=[s_["Ak"], s_["vk"]], w=[bk])
                        cb_, cbk = self.pb[7], "pb7"
                        cc = (i % 2) * 128
                        b.mm(cb_[:, cc:cc + 128], lhsT=s_["qkT"][:, 0:128], rhs=Rb[:], start=True, stop=True,
                             r=[s_["qkTk"], "Rb"], w=[cbk])
                        kb_, kbk = self.pb[6], "pb6"
                        ck = (i % 2) * 128
                        b.mm(kb_[:, ck:ck + 128], lhsT=s_["kd"][:], rhs=s_["vs"][:], start=True, stop=True,
                             r=[s_["kdk"], s_["vk"]], w=[kbk])
                        b.stt("dve", R[:], R[:], g128, kb_[:, ck:ck + 128], ALU.mult, ALU.add, r=["R", kbk, "decs"], w=["R"])
                        b.cp("act", Rb[:], R[:], r=["R"], w=["Rb"])
                        ti, tik = ti_r.next()
                        b.cp("act", ti[:], bank[:, c:c + 128], r=[bk], w=[tik])
                        if d == 0:
                            b.stt("dve", out_f[:, t, :], cb_[:, cc:cc + 128], qdec, ti[:], ALU.mult, ALU.add,
                                  r=[cbk, tik, "decs"], w=[("outf", t)])
                        else:
                            tot, totk = tot_r.next()
                            b.stt("dve", tot[:], cb_[:, cc:cc + 128], qdec, ti[:], ALU.mult, ALU.add,
                                  r=[cbk, tik, "decs"], w=[totk])
                            b.tt("dve", tot[:], tot[:], out_f[:, t, :], ALU.add, r=[totk, ("outf", t)], w=[totk])
                            s_["tot"], s_["totk"] = tot, totk

                    def stage4(i):
                        s_ = stt_[i]
                        t = s_["t"]
                        gi = s_["gi"]
                        tot, totk = s_["tot"], s_["totk"]
                        ssq, sqk = ssq_r.next()
                        b.memset("dve", ssq[:], 0.0, w=[sqk])
                        jk, jkk = junk_r.next()
                        b.act(jk[:], tot[:], AF.Square, r=[totk, sqk], w=[jkk, sqk], accum_out=ssq[:, 0:1])
                        self.rstd_from_psum(ssq[:, 1:2], sqk, ssq[:, 0:1], sqk, 1.0 / 128)
                        on, onk = on_r.next()
                        b.ts("dve", on[:], tot[:], ssq[:, 1:2], None, ALU.mult, None, r=[totk, sqk], w=[onk])
                        bank, bk = self.pb[7], "pb7"
                        c = 256 + (i % 2) * 128
                        b.mm(bank[:, c:c + 128], lhsT=on[:], rhs=self.ident_bf[:], start=True, stop=True, r=[onk, "identbf"],
                             w=[bk])
                        b.tt("dve", ogT[:, t * 128:(t + 1) * 128], bank[:, c:c + 128], sgT[:, t * 128:(t + 1) * 128], ALU.mult,
                             r=[bk, ("sgT", gi)], w=[("ogT", gi)])

                    order = order[:int(os.environ.get("RET_N", "18"))]
                    n = len(order)
                    nst = 5 if d == 1 else 4
                    nst = min(nst, int(os.environ.get("RET_S", "5")))
                    for step in range(n + nst - 1):
                        if step < n:
                            stage0(step)
                        if nst > 1 and 0 <= step - 1 < n:
                            stage1(step - 1)
                        if nst > 2 and 0 <= step - 2 < n:
                            stage2(step - 2)
                        if nst > 3 and 0 <= step - 3 < n:
                            stage3(step - 3)
                        if nst > 4 and d == 1 and 0 <= step - 4 < n:
                            stage4(step - 4)
                            stt_.pop(step - 4)
                for gi, (g0, N) in enumerate(GROUPS):
                    b.dma("sp", self.og_scr[h][:, g0:g0 + N], ogT[:, g0:g0 + N], r=[("ogT", gi)], w=[("ogs", h, gi)])
            b.barrier()

    def phase_merge(self, bi, l):
        b = self
        SG = [[0, 1], [2, 3], [4]]
        with contextlib.ExitStack() as st:
            ogg = b.sb("ogg", [128, 12, 1024], BF16, st)
            mer = b.sb("mer", [128, 8, 1024], BF16, st)
            wout = b.sb("wout", [128, 8, 1024], BF16, st)
            wm_r = Ring(b, "wm", 2, [128, 4608], BF16, st)
            m_r = Ring(b, "mm", 2, [128, 512], F32, st)
            acc_r = Ring(b, "acc", 2, [128, 512], F32, st)
            tmp_r = Ring(b, "mtmp", 2, [128, 512], F32, st)
            b.dma("sp", wout[:], self.s_out[l], r=self.ckeys[("s_out", l)], w=["wout"])
            cnt = {"b": 0, "m": 0, "o": 0}
            for gis in SG:
                t0 = GROUPS[gis[0]][0]
                nsg = sum(GROUPS[g][1] for g in gis)
                for c in range(12):
                    b.dma("sp", ogg[:, c, 0:nsg], self.og_scr[c][:, t0:t0 + nsg],
                          r=[("ogs", c, gi) for gi in gis], w=[("ogg", c)])
                for dc in range(8):
                    wm, wmk = wm_r.next()
                    b.dma("sp", wm[:], self.s_mrg[l, dc], r=self.ckeys[("s_mrg", l, dc)], w=[wmk])
                    wmg = wm[:, 0:3072].rearrange("p (kc b n) -> p kc b n", kc=8, b=3)
                    wp = wm[:, 3072:4608].rearrange("p (b kc n) -> p b kc n", b=3, kc=4)
                    for gi in gis:
                        g0, N = GROUPS[gi]
                        o0 = g0 - t0
                        acc, ack = acc_r.next()
                        for br in range(3):
                            cnt["b"] += 1
                            bb_, bbk = self.pb[cnt["b"] % 3], "pb%d" % (cnt["b"] % 3)
                            for kc in range(4):
                                b.mm(bb_[:, :N], lhsT=wp[:, br, kc, :], rhs=ogg[:, br * 4 + kc, o0:o0 + N],
                                     start=(kc == 0), stop=(kc == 3), r=[wmk, ("ogg", br * 4 + kc)], w=[bbk])
                            cnt["m"] += 1
                            mb_, mbk = self.pb[3 + cnt["m"] % 3], "pb%d" % (3 + cnt["m"] % 3)
                            for kc in range(8):
                                b.mm(mb_[:, :N], lhsT=wmg[:, kc, br, :], rhs=self.hT[:, kc, g0:g0 + N],
                                     start=(kc == 0), stop=(kc == 7), r=[wmk, ("hT", gi)], w=[mbk])
                            m, mk = m_r.next()
                            b.act(m[:, :N], mb_[:, :N], AF.Sigmoid, r=[mbk], w=[mk])
                            if br == 0:
                                b.tt("dve", acc[:, :N], bb_[:, :N], m[:, :N], ALU.mult, r=[bbk, mk], w=[ack])
                            else:
                                tp, tk = tmp_r.next()
                                b.tt("dve", tp[:, :N], bb_[:, :N], m[:, :N], ALU.mult, r=[bbk, mk], w=[tk])
                                if br == 1:
                                    b.tt("dve", acc[:, :N], acc[:, :N], tp[:, :N], ALU.add, r=[ack, tk], w=[ack])
                                else:
                                    b.tt("dve", mer[:, dc, o0:o0 + N], acc[:, :N], tp[:, :N], ALU.add,
                                         r=[ack, tk], w=[("mer", dc)])
                for do in range(8):
                    for gi in gis:
                        g0, N = GROUPS[gi]
                        o0 = g0 - t0
                        j = 2 if gi == 0 else bi
                        cnt["o"] += 1
                        ob, obk = self.pb[6 + cnt["o"] % 2], "pb%d" % (6 + cnt["o"] % 2)
                        for dc in range(8):
                            b.mm(ob[:, :N], lhsT=wout[:, dc, do * 128:(do + 1) * 128], rhs=mer[:, dc, o0:o0 + N],
                                 start=(dc == 0), stop=(dc == 7), r=["wout", ("mer", dc)], w=[obk])
                        b.stt("dve", self.xT[:, do, g0:g0 + N], ob[:, :N], self.modv[:, l, j, 2, do:do + 1], self.xT[:, do, g0:g0 + N],
                              ALU.mult, ALU.add, r=[obk, "modv", ("xT", gi)], w=[("xT", gi)])
            b.barrier()


def _const_tables():
    f32 = np.float32
    ident = np.eye(128, dtype=f32)
    swap = np.zeros((128, 128), f32)
    for m in range(128):
        swap[m ^ 1, m] = 1.0
    ones = np.ones((128, 128), f32)
    bones = np.zeros((128, 128), f32)
    bones[0:64, 0:64] = 1.0
    bones[64:128, 64:128] = 1.0
    cst = np.stack([ident, swap, ones, bones], axis=1)
    p = np.arange(S)
    rows = (p // 64).astype(f32)
    cols = (p % 64).astype(f32)
    freqs = (np.float32(10000.0) ** (-np.arange(16, dtype=f32) / np.float32(16))).astype(f32)
    ang = np.concatenate([rows[:, None] * freqs[None], cols[:, None] * freqs[None]], axis=-1).astype(f32)
    q = np.arange(128)
    i_of_q = (q % 64) // 2
    sign = np.where((q % 2) == 0, -1.0, 1.0)
    a = ang.astype(np.float64)[:, i_of_q].T
    axtab = np.stack([np.cos(a), np.sin(a) * sign[:, None]], axis=1).astype(f32)
    theta = (np.float32(10000.0) ** (-np.linspace(0.0, 1.0, 64, dtype=f32))).astype(f32)
    pos = np.arange(20 * 128, dtype=f32)
    ra = (pos[:, None] * theta[None]).astype(f32).astype(np.float64)
    rc = np.cos(ra).reshape(20, 128, 64).transpose(1, 0, 2)
    rs = np.sin(ra).reshape(20, 128, 64).transpose(1, 0, 2)
    rtab = np.stack([rc, rs], axis=1).astype(f32)
    aa = np.arange(128)[:, None]
    bb = np.arange(128)[None, :]
    wmask = np.concatenate([(aa <= bb), np.ones((128, 128), bool), (bb <= aa)], axis=1).astype(f32)
    s_ = np.arange(128)[:, None]
    t_ = np.arange(128)[None, :]
    sc = np.float32(128 ** -0.5)
    dtab = np.stack([np.maximum(t_ - s_, 0), np.maximum(s_ - t_, 0), (t_ >= s_) * sc, (s_ > t_) * sc], axis=1).astype(f32)
    return cst, axtab, rtab, wmask, dtab


def _params(core, c, c_ctx, b_mod, norm_w, final_norm_w, ax_q_gain, ax_k_gain, win_sink, ret_decay_fwd, ret_decay_bwd):
    f32 = np.float32
    par = np.zeros((128, NPAR), f32)
    cs = np.stack([c[2 * core], c[2 * core + 1], c_ctx], axis=0)
    par[:, P_CT:P_CT + 24] = cs.reshape(3, 8, 128).transpose(2, 1, 0).reshape(128, 24)
    par[:, P_BMOD:P_BMOD + 96] = b_mod.reshape(DEPTH, 24, 128).transpose(2, 0, 1).reshape(128, 96)
    par[:, P_NORMW:P_NORMW + 32] = norm_w.reshape(DEPTH, 8, 128).transpose(2, 0, 1).reshape(128, 32)
    par[:, P_FNW:P_FNW + 8] = final_norm_w.reshape(8, 128).T
    idx = np.arange(128) % 64
    par[:, P_QG:P_QG + 4] = ax_q_gain[:, idx].T
    par[:, P_KG:P_KG + 4] = ax_k_gain[:, idx].T
    par[:, P_SINK:P_SINK + 32] = np.broadcast_to(win_sink.reshape(1, 32), (128, 32))
    dec = np.stack([ret_decay_fwd, ret_decay_bwd], axis=1)
    par[:, P_DEC:P_DEC + 32] = np.broadcast_to(dec.reshape(1, 32), (128, 32))
    par[:, P_EPS] = EPS
    pp = np.arange(128, dtype=f32)
    pexp = np.zeros((128, 3, 8), f32)
    pexp[:, 0, 0:4] = (pp + 1)[:, None]
    pexp[:, 0, 4:8] = (128 - pp)[:, None]
    pexp[:, 1, 0:4] = (127 - pp)[:, None]
    pexp[:, 1, 4:8] = pp[:, None]
    pexp[:, 2, :] = 128.0
    par[:, P_PEXP:P_PEXP + 24] = pexp.reshape(128, 24)
    return par


_CACHE = {}


def _get_prog(**kw):
    key = tuple(sorted(kw.items()))
    if key not in _CACHE:
        _CACHE[key] = Prog(**kw)
    return _CACHE[key]


def _run(inputs, **kw):
    f32 = np.float32
    g = lambda k: np.asarray(inputs[k], dtype=f32)
    x, c, ctx, c_ctx = g("x"), g("c"), g("ctx"), g("c_ctx")
    w_in = np.array(g("w_in"), dtype=f32, copy=True)
    perm = np.concatenate([np.arange(0, 128, 2), np.arange(1, 128, 2)])
    for c0 in ():
        for h in range(4):
            blk = w_in[:, :, c0 + h * 128:c0 + (h + 1) * 128].copy()
            w_in[:, :, c0 + h * 128:c0 + (h + 1) * 128] = blk[:, :, perm]
    cst, axtab, rtab, wmask, dtab = _const_tables()
    wk = w_in[:, :, C_WK:C_WK + 128]
    ak = w_in[:, :, C_AK:C_AK + 128]
    w_kd = np.ascontiguousarray(np.concatenate(
        [wk[..., 0:64], wk[..., 0:64], wk[..., 64:128], wk[..., 64:128],
         ak[..., 0:64], ak[..., 0:64], ak[..., 64:128], ak[..., 64:128]], axis=-1))
    w_p = np.ascontiguousarray(np.stack([g("w_proj_ret"), g("w_proj_win"), g("w_proj_ax")], axis=1))
    w_mod = np.ascontiguousarray(g("w_mod"))
    w_out = np.ascontiguousarray(g("w_out"))
    prog = _get_prog(**kw)
    in_maps = []
    for core in range(NCORES):
        xin = np.ascontiguousarray(np.concatenate([ctx[2 * core:2 * core + 2], x[2 * core:2 * core + 2]], axis=1))
        par = _params(core, c, c_ctx, g("b_mod"), g("norm_w"), g("final_norm_w"), g("ax_q_gain"), g("ax_k_gain"),
                      g("win_sink"), g("ret_decay_fwd"), g("ret_decay_bwd"))
        in_maps.append({"xin": xin, "par": par, "cst": cst, "axtab": axtab, "rtab": rtab, "wmask": wmask, "dtab": dtab,
                        "w_in": w_in, "w_kd": w_kd, "w_mod": w_mod, "w_p": w_p, "w_out": w_out})
    res = run_bass_kernel_spmd(prog.nc, in_maps, core_ids=list(range(NCORES)))
    global LAST_RES
    LAST_RES = res
    out = np.concatenate([np.asarray(r["y"]) for r in res.results], axis=0)
    return out.astype(np.float32)


def kernel(**inputs):
    return _run(inputs)
```

```python
import contextlib
import numpy as np
import concourse.bass as bass
import concourse.mybir as mybir
from concourse.bass_utils import run_bass_kernel_spmd

F32 = mybir.dt.float32
BF16 = mybir.dt.bfloat16
AF = mybir.ActivationFunctionType
ALU = mybir.AluOpType

D = 1024
L = 256
S = 2048
TOK = L + S
NT = TOK // 128
DEPTH = 4
NCORES = 8
BPC = 2
GROUPS = [(0, 256), (256, 512), (768, 512), (1280, 512), (1792, 512)]
EPS = 1e-6
INW = 7680
C_RQ, C_RK, C_RV, C_RG = 0, 512, 1024, 1536
C_WQ, C_WK, C_WV, C_WG = 2048, 2560, 2688, 2816
C_AQ, C_AK, C_AV, C_AG = 3328, 3840, 3968, 4096
C_MG = 4608

P_CT = 0
P_BMOD = 24
P_NORMW = 120
P_FNW = 152
P_QG = 160
P_KG = 164
P_SINK = 168
P_DEC = 200
P_EPS = 232
P_PEXP = 233
NPAR = 257


import os as _os
NOWAR = bool(int(_os.environ.get("NOWAR", "0")))


class _Res:
    __slots__ = ("w", "r")

    def __init__(self):
        self.w = None
        self.r = {}


class Builder:
    ND = 40
    QSLOTS = {"sp": list(range(0, 24)), "pool": list(range(24, 40)), "act": list(range(0, 24))}

    def __init__(self):
        self.nc = bass.Bass("TRN2", target_bir_lowering=False)
        nc = self.nc
        self.es = contextlib.ExitStack()
        self.eng = {"pe": nc.tensor, "dve": nc.vector, "act": nc.scalar, "pool": nc.gpsimd, "sp": nc.sync}
        self.sem = {}
        self.cnt = {}
        for e in self.eng:
            self.sem[e] = self.es.enter_context(nc.semaphore("sem_" + e))
            self.cnt[e] = 0
        for k in range(self.ND):
            self.sem[("d", k)] = self.es.enter_context(nc.semaphore("dsem%d" % k))
            self.cnt[("d", k)] = 0
        self.seen = {e: {} for e in self.eng}
        self.res = {}
        self.dslot = {"sp": 0, "pool": 0, "act": 0}
        self.n_inst = 0
        self.uid = 0

    def sb(self, name, shape, dt, stack=None):
        self.uid += 1
        return (stack or self.es).enter_context(self.nc.sbuf_tensor("%s_%d" % (name, self.uid), list(shape), dt))

    def ps(self, name, shape, dt):
        return self.es.enter_context(self.nc.psum_tensor(name, list(shape), dt))

    def dram(self, name, shape, dt, kind):
        return self.nc.dram_tensor(name, list(shape), dt, kind=kind).ap()

    def _deps(self, e, reads, writes):
        deps = {}
        res = self.res
        for k in reads:
            R = res.get(k)
            if R is None:
                R = res[k] = _Res()
            if R.w is not None:
                f, n = R.w
                if not (f == e and e == "pe"):
                    if deps.get(f, 0) < n:
                        deps[f] = n
        for k in writes:
            R = res.get(k)
            if R is None:
                R = res[k] = _Res()
            if R.w is not None:
                f, n = R.w
                if not (f == e and e == "pe"):
                    if deps.get(f, 0) < n:
                        deps[f] = n
            for f, n in R.r.items():
                if f == e and (e == "pe" or NOWAR):
                    continue
                if deps.get(f, 0) < n:
                    deps[f] = n
        return deps

    def _wait(self, e, deps):
        seen = self.seen[e]
        eng = self.eng[e]
        for f, n in deps.items():
            if seen.get(f, 0) < n:
                eng.wait_ge(self.sem[f], n)
                seen[f] = n

    def _record(self, who, n, reads, writes):
        res = self.res
        for k in reads:
            res[k].r[who] = n
        for k in writes:
            R = res[k]
            R.w = (who, n)
            R.r = {}

    def op(self, e, emit, r=(), w=()):
        deps = self._deps(e, r, w)
        self._wait(e, deps)
        inst = emit()
        self.cnt[e] += 1
        n = self.cnt[e]
        inst.then_inc(self.sem[e], 1)
        self._record(e, n, r, w)
        self.n_inst += 1
        return inst

    def dma(self, q, out, in_, r=(), w=()):
        deps = self._deps(q, r, w)
        sl = self.QSLOTS[q]
        k = sl[self.dslot[q] % len(sl)]
        self.dslot[q] += 1
        who = ("d", k)
        if self.cnt[who] > 0:
            deps[who] = max(deps.get(who, 0), self.cnt[who])
        self._wait(q, deps)
        inst = self.eng[q].dma_start(out=out, in_=in_)
        self.cnt[who] += 16
        inst.then_inc(self.sem[who], 16)
        self._record(who, self.cnt[who], r, w)
        self.n_inst += 1
        return inst

    def barrier(self, final=False):
        for e in self.eng:
            deps = {f: n for f, n in self.cnt.items() if n > 0 and f != e
                    and not (final is False and isinstance(f, tuple) and f[1] in self.QSLOTS["pool"])}
            self._wait(e, deps)

    def mm(self, out, lhsT, rhs, start, stop, r, w):
        return self.op("pe", lambda: self.nc.tensor.matmul(out, lhsT=lhsT, rhs=rhs, start=start, stop=stop), r, w)

    def act(self, out, in_, func, r, w, bias=None, scale=None, accum_out=None):
        kw = {}
        if bias is not None:
            kw["bias"] = bias
        if scale is not None:
            kw["scale"] = scale
        if accum_out is not None:
            kw["accum_out"] = accum_out
        return self.op("act", lambda: self.nc.scalar.activation(out=out, in_=in_, func=func, **kw), r, w)

    def tt(self, e, out, in0, in1, op, r, w):
        return self.op(e, lambda: self.eng[e].tensor_tensor(out=out, in0=in0, in1=in1, op=op), r, w)

    def ts(self, e, out, in0, s1, s2, op0, op1, r, w):
        if s2 is None:
            return self.op(e, lambda: self.eng[e].tensor_scalar(out=out, in0=in0, scalar1=s1, scalar2=None, op0=op0), r, w)
        return self.op(e, lambda: self.eng[e].tensor_scalar(out=out, in0=in0, scalar1=s1, scalar2=s2, op0=op0, op1=op1), r, w)

    def stt(self, e, out, in0, scalar, in1, op0, op1, r, w):
        return self.op(e, lambda: self.eng[e].scalar_tensor_tensor(out=out, in0=in0, scalar=scalar, in1=in1, op0=op0, op1=op1), r, w)

    def cp(self, e, out, in_, r, w):
        if e == "act":
            return self.op(e, lambda: self.nc.scalar.copy(out=out, in_=in_), r, w)
        return self.op(e, lambda: self.eng[e].tensor_copy(out=out, in_=in_), r, w)

    def recip(self, out, in_, r, w):
        return self.op("dve", lambda: self.nc.vector.reciprocal(out=out, in_=in_), r, w)

    def memset(self, e, ap, val, w):
        return self.op(e, lambda: self.eng[e].memset(ap, val), (), w)


class Ring:
    def __init__(self, b, name, n, shape, dt, stack=None):
        self.bufs = []
        for i in range(n):
            t = b.sb(name, shape, dt, stack)
            b.uid += 1
            self.bufs.append((t, "%s#%d" % (name, b.uid)))
        self.i = 0

    def next(self):
        t = self.bufs[self.i]
        self.i = (self.i + 1) % len(self.bufs)
        return t


class Prog(Builder):
    def __init__(self, n_layers=DEPTH, n_batch=BPC, stage=99, dbg=False, branches=("ax", "win", "ret")):
        super().__init__()
        self.n_layers = n_layers
        self.n_batch = n_batch
        self.stage = stage
        self.branches = branches
        self.dbg = dbg
        self.marks = []
        self.build()

    def mark(self, name):
        self.marks.append((name, dict((k, v) for k, v in self.cnt.items() if not isinstance(k, tuple))))

    def build(self):
        b = self
        self.xin = b.dram("xin", [BPC, TOK, D], F32, "ExternalInput")
        self.par_d = b.dram("par", [128, NPAR], F32, "ExternalInput")
        self.cst_d = b.dram("cst", [128, 4, 128], F32, "ExternalInput")
        self.axtab_d = b.dram("axtab", [128, 2, S], F32, "ExternalInput")
        self.rtab_d = b.dram("rtab", [128, 2, 20, 64], F32, "ExternalInput")
        self.wmask_d = b.dram("wmask", [128, 384], F32, "ExternalInput")
        self.dtab_d = b.dram("dtab", [128, 4, 128], F32, "ExternalInput")
        self.w_in = b.dram("w_in", [DEPTH, D, INW], F32, "ExternalInput")
        self.w_kd = b.dram("w_kd", [DEPTH, D, 512], F32, "ExternalInput")
        self.w_mod = b.dram("w_mod", [DEPTH, D, 3 * D], F32, "ExternalInput")
        self.w_p = b.dram("w_p", [DEPTH, 3, 512, D], F32, "ExternalInput")
        self.w_out = b.dram("w_out", [DEPTH, D, D], F32, "ExternalInput")
        self.y = b.dram("y", [BPC, S, D], F32, "ExternalOutput")
        self.og_scr = b.dram("og_scr", [12, 128, TOK], BF16, "ExternalOutput" if self.dbg else "Internal")
        self.s_attn = b.dram("s_attn", [DEPTH, 2, 128, 8, 1408], BF16, "Internal")
        self.s_ret = b.dram("s_ret", [DEPTH, 4, 128, 8, 512], BF16, "Internal")
        self.s_mrg = b.dram("s_mrg", [DEPTH, 8, 128, 4608], BF16, "Internal")
        self.s_out = b.dram("s_out", [DEPTH, 128, 8, 1024], BF16, "Internal")
        if self.dbg:
            self.hT_d = b.dram("hT_d", [128, 8, TOK], BF16, "ExternalOutput")

        self.xT = b.sb("xT", [128, 8, TOK], F32)
        self.hT = b.sb("hT", [128, 8, TOK], BF16)
        self.par = b.sb("par", [128, NPAR], F32)
        self.cst = b.sb("cst", [128, 4, 128], F32)
        self.ident_bf = b.sb("identbf", [128, 128], BF16)
        self.wmask = b.sb("wmask", [128, 384], BF16)
        self.dtab = b.sb("dtab", [128, 4, 128], F32)
        self.modv = b.sb("modv", [128, DEPTH, 3, 3, 8], F32)
        self.lg = b.sb("lg", [128, 32], F32)
        self.esink = b.sb("esink", [128, 32], F32)
        self.Dm = b.sb("Dm", [128, 8, 128], BF16)
        self.decs = b.sb("decs", [128, 3, 8], F32)
        self.pb = [b.ps("pb%d" % i, [128, 512], F32) for i in range(8)]
        self.ident_f = self.cst[:, 0, :]
        self.swap_f = self.cst[:, 1, :]
        self.ones_f = self.cst[:, 2, :]
        self.bones_f = self.cst[:, 3, :]
        self.eps_ap = self.par[:, P_EPS:P_EPS + 1]

        self.convert(0)
        self.prologue()
        for bi in range(self.n_batch):
            self.load_x(bi)
            for l in range(self.n_layers):
                if bi == 0 and l + 1 < self.n_layers:
                    self.convert(l + 1)
                self.layer(bi, l)
            if self.stage >= 99:
                self.final(bi)
            else:
                self.dump_xT(bi)
        self.barrier(final=True)

    def pcol(self, c, n=1):
        return self.par[:, c:c + n]

    def convert(self, l):
        b = self

        if not hasattr(self, "ckeys"):
            self.ckeys = {}

        def cv(dst, src, key):
            lst = self.ckeys.setdefault(key, [])
            k2 = key + (len(lst),)
            lst.append(k2)
            b.dma("pool", dst, src.rearrange("(kc p) n -> p kc n", p=128), w=[k2])

        w = self.w_in[l]
        for ki, (cq, cg, cv_, kd0) in enumerate(((C_WQ, C_WG, C_WV, 0), (C_AQ, C_AG, C_AV, 256))):
            key = ("s_attn", l, ki)
            dst = self.s_attn[l, ki]
            cv(dst[:, :, 0:512], w[:, cq:cq + 512], key)
            cv(dst[:, :, 512:1024], w[:, cg:cg + 512], key)
            cv(dst[:, :, 1024:1280], self.w_kd[l][:, kd0:kd0 + 256], key)
            cv(dst[:, :, 1280:1408], w[:, cv_:cv_ + 128], key)
        for h in range(4):
            key = ("s_ret", l, h)
            dst = self.s_ret[l, h]
            for i, c0 in enumerate((C_RQ, C_RK, C_RV, C_RG)):
                cv(dst[:, :, i * 128:(i + 1) * 128], w[:, c0 + h * 128:c0 + (h + 1) * 128], key)
        for dc in range(8):
            key = ("s_mrg", l, dc)
            dmg = self.s_mrg[l, dc][:, 0:3072].rearrange("p (kc b n) -> p kc b n", kc=8, b=3)
            dwp = self.s_mrg[l, dc][:, 3072:4608].rearrange("p (b kc n) -> p b kc n", b=3, kc=4)
            for br in range(3):
                c0 = C_MG + br * 1024 + dc * 128
                cv(dmg[:, :, br, :], w[:, c0:c0 + 128], key)
                cv(dwp[:, br], self.w_p[l, br][:, dc * 128:(dc + 1) * 128], key)
        cv(self.s_out[l], self.w_out[l], ("s_out", l))

    def prologue(self):
        b = self
        b.dma("sp", self.par[:], self.par_d, w=["par"])
        b.dma("sp", self.cst[:], self.cst_d, w=["cst"])
        b.dma("sp", self.dtab[:], self.dtab_d, w=["dtab"])
        with contextlib.ExitStack() as st:
            wmf = b.sb("wmf", [128, 384], F32, st)
            b.dma("sp", wmf[:], self.wmask_d, w=["wmf"])
            b.cp("dve", self.wmask[:], wmf[:], r=["wmf"], w=["wmask"])
            b.cp("dve", self.ident_bf[:], self.ident_f, r=["cst"], w=["identbf"])
            t1 = b.sb("plt", [128, 32], F32, st)
            b.act(t1[:], self.pcol(P_DEC, 32), AF.Exp, r=["par"], w=["plt"], scale=-1.0)
            b.act(t1[:], t1[:], AF.Ln, r=["plt"], w=["plt"], bias=1.0)
            b.ts("dve", self.lg[:], t1[:], -1.0, None, ALU.mult, None, r=["plt"], w=["lg"])
            b.act(self.esink[:], self.pcol(P_SINK, 32), AF.Exp, r=["par"], w=["esink"])
            sc = b.sb("sc", [128, 24], F32, st)
            b.act(sc[:], self.pcol(P_CT, 24), AF.Silu, r=["par"], w=["sc"])
            wring = Ring(b, "wmodc", 2, [128, 3 * D], F32, st)
            modraw = b.sb("modraw", [128, 24, 3], F32, st)
            for l in range(self.n_layers):
                bank = self.pb[l % 2]
                bk = "pb%d" % (l % 2)
                for kc in range(8):
                    wt, wk = wring.next()
                    b.dma("sp", wt[:], self.w_mod[l, kc * 128:(kc + 1) * 128, :], w=[wk])
                    for ch in range(24):
                        b.mm(bank[:, ch * 3:ch * 3 + 3], lhsT=wt[:, ch * 128:(ch + 1) * 128],
                             rhs=sc[:, kc * 3:kc * 3 + 3], start=(kc == 0 and ch == 0), stop=(kc == 7 and ch == 23),
                             r=[wk, "sc"], w=[bk])
                bm = self.par[:, P_BMOD + l * 24:P_BMOD + (l + 1) * 24]
                b.tt("dve", modraw[:], bank[:, 0:72].rearrange("p (c j) -> p c j", j=3),
                     bm.unsqueeze(2).to_broadcast([128, 24, 3]), ALU.add, r=[bk, "par"], w=["modraw"])
                nw = self.par[:, P_NORMW + l * 8:P_NORMW + (l + 1) * 8]
                for j in range(3):
                    b.stt("dve", self.modv[:, l, j, 0, :], modraw[:, 8:16, j], 1.0, nw, ALU.add, ALU.mult,
                          r=["modraw", "par"], w=["modv"])
                    b.cp("dve", self.modv[:, l, j, 1, :], modraw[:, 0:8, j], r=["modraw"], w=["modv"])
                    b.cp("dve", self.modv[:, l, j, 2, :], modraw[:, 16:24, j], r=["modraw"], w=["modv"])
            b.barrier()

    def load_x(self, bi):
        b = self
        with contextlib.ExitStack() as st:
            xs_ring = Ring(b, "xs", 2, [128, D], F32, st)
            for t in range(NT):
                xs, xk = xs_ring.next()
                b.dma("sp", xs[:], self.xin[bi, t * 128:(t + 1) * 128, :], w=[xk])
                gi = self.gi_of_tile(t)
                for half in range(2):
                    bank = self.pb[(2 * t + half) % 4]
                    bk = "pb%d" % ((2 * t + half) % 4)
                    for q in range(4):
                        kc = half * 4 + q
                        b.mm(bank[:, q * 128:(q + 1) * 128], lhsT=xs[:, kc * 128:(kc + 1) * 128], rhs=self.ident_f,
                             start=True, stop=True, r=[xk, "cst"], w=[bk])
                    e = "act" if half == 0 else "dve"
                    b.cp(e, self.xT[:, half * 4:(half + 1) * 4, t * 128:(t + 1) * 128],
                         bank[:, :].rearrange("p (q n) -> p q n", q=4), r=[bk], w=[("xT", gi)])
            b.barrier()

    @staticmethod
    def gi_of_tile(t):
        return 0 if t < 2 else 1 + (t - 2) // 4

    def rstd_from_psum(self, out_ap, okey, in_ap, ikey, inv_n):
        b = self
        b.act(out_ap, in_ap, AF.Ln, r=[ikey, "par"], w=[okey], scale=inv_n, bias=self.eps_ap)
        b.act(out_ap, out_ap, AF.Exp, r=[okey], w=[okey], scale=-0.5)

    def rms_rstd(self, st_rings, src, g0, N, gi, inv_n, bank, bk):
        b = self
        sq_ring, rs_ring = st_rings
        for kc in range(8):
            sq, sk = sq_ring.next()
            b.act(sq[:, :N], src[:, kc, g0:g0 + N], AF.Square, r=[("xT", gi)], w=[sk])
            b.mm(bank[:, :N], lhsT=self.ones_f, rhs=sq[:, :N], start=(kc == 0), stop=(kc == 7), r=[sk, "cst"], w=[bk])
        rs, rk = rs_ring.next()
        self.rstd_from_psum(rs[:, :N], rk, bank[:, :N], bk, inv_n)
        return rs, rk

    def final(self, bi):
        b = self
        with contextlib.ExitStack() as st:
            rings = (Ring(b, "fsq", 3, [128, 512], F32, st), Ring(b, "frs", 2, [128, 512], F32, st))
            yT = b.sb("yT", [128, 8, 512], F32, st)
            ys_ring = Ring(b, "ys", 2, [128, D], F32, st)
            for gi in range(1, 5):
                g0, N = GROUPS[gi]
                rs, rk = self.rms_rstd(rings, self.xT, g0, N, gi, 1.0 / D, self.pb[gi % 2], "pb%d" % (gi % 2))
                for kc in range(8):
                    b.stt("dve", yT[:, kc, :], self.xT[:, kc, g0:g0 + N], self.pcol(P_FNW + kc), rs[:, :N],
                          ALU.mult, ALU.mult, r=[("xT", gi), rk, "par"], w=[("yT", kc)])
                for tt in range(4):
                    ys, yk = ys_ring.next()
                    for half in range(2):
                        bank = self.pb[2 + (2 * tt + half) % 4]
                        bk = "pb%d" % (2 + (2 * tt + half) % 4)
                        for q in range(4):
                            kc = half * 4 + q
                            b.mm(bank[:, q * 128:(q + 1) * 128], lhsT=yT[:, kc, tt * 128:(tt + 1) * 128],
                                 rhs=self.ident_f, start=True, stop=True, r=[("yT", kc), "cst"], w=[bk])
                        e = "act" if half == 0 else "dve"
                        b.cp(e, ys[:, half * 512:(half + 1) * 512], bank[:, :], r=[bk], w=[yk])
                    p0 = g0 - L + tt * 128
                    b.dma("sp", self.y[bi, p0:p0 + 128, :], ys[:], r=[yk], w=["y_out"])
            b.barrier()

    def dump_xT(self, bi):
        b = self
        with contextlib.ExitStack() as st:
            ys_ring = Ring(b, "ys", 2, [128, D], F32, st)
            for t in range(2, NT):
                ys, yk = ys_ring.next()
                gi = self.gi_of_tile(t)
                for half in range(2):
                    bank = self.pb[2 + (2 * t + half) % 4]
                    bk = "pb%d" % (2 + (2 * t + half) % 4)
                    for q in range(4):
                        kc = half * 4 + q
                        b.mm(bank[:, q * 128:(q + 1) * 128], lhsT=self.xT[:, kc, t * 128:(t + 1) * 128],
                             rhs=self.ident_f, start=True, stop=True, r=[("xT", gi), "cst"], w=[bk])
                    b.cp("act" if half == 0 else "dve", ys[:, half * 512:(half + 1) * 512], bank[:, :], r=[bk], w=[yk])
                b.dma("sp", self.y[bi, (t - 2) * 128:(t - 1) * 128, :], ys[:], r=[yk], w=["y_out"])
            b.barrier()

    def layer(self, bi, l):
        self.mark("L%d.%d tables" % (bi, l))
        self.layer_tables(l)
        self.mark("L%d.%d norm" % (bi, l))
        self.phase_norm(bi, l)
        for name, br in (("ax", 2), ("win", 1), ("ret", 0)):
            if name not in self.branches:
                self.zero_og(br)
        if "ax" in self.branches:
            self.mark("L%d.%d ax" % (bi, l))
            self.phase_attn(bi, l, "ax")
        if "win" in self.branches:
            self.mark("L%d.%d win" % (bi, l))
            self.phase_attn(bi, l, "win")
        if "ret" in self.branches:
            self.mark("L%d.%d ret" % (bi, l))
            self.phase_ret(bi, l)
        self.mark("L%d.%d merge" % (bi, l))
        self.phase_merge(bi, l)
        self.mark("L%d.%d end" % (bi, l))

    def zero_og(self, br):
        b = self
        with contextlib.ExitStack() as st:
            z = b.sb("zog", [128, TOK], BF16, st)
            b.memset("dve", z[:], 0.0, w=["zog"])
            for c in range(4):
                for gi, (g0, N) in enumerate(GROUPS):
                    b.dma("sp", self.og_scr[br * 4 + c][:, g0:g0 + N], z[:, g0:g0 + N], r=["zog"], w=[("ogs", br * 4 + c, gi)])
            b.barrier()

    def layer_tables(self, l):
        b = self
        with contextlib.ExitStack() as st:
            tmp = b.sb("ltmp", [128, 128], F32, st)
            t8 = b.sb("lt8", [128, 3, 8], F32, st)
            lg8 = self.lg[:, l * 8:(l + 1) * 8]
            for row in range(3):
                b.tt("dve", t8[:, row, :], self.par[:, P_PEXP + row * 8:P_PEXP + (row + 1) * 8], lg8, ALU.mult,
                     r=["par", "lg"], w=["lt8"])
            b.act(self.decs[:].rearrange("p a b -> p (a b)"), t8[:].rearrange("p a b -> p (a b)"), AF.Exp,
                  r=["lt8"], w=["decs"])
            b.ts("dve", self.decs[:, 1, :], self.decs[:, 1, :], float(128 ** -0.5), None, ALU.mult, None,
                 r=["decs"], w=["decs"])
            for d in range(2):
                for h in range(4):
                    i = d * 4 + h
                    b.act(tmp[:], self.dtab[:, d, :], AF.Exp, r=["dtab", "lg"], w=["ltmp"], scale=self.lg[:, l * 8 + i:l * 8 + i + 1])
                    b.tt("dve", self.Dm[:, i, :], tmp[:], self.dtab[:, 2 + d, :], ALU.mult, r=["ltmp", "dtab"], w=["Dm"])
            b.barrier()

    def phase_norm(self, bi, l):
        b = self
        with contextlib.ExitStack() as st:
            rings = (Ring(b, "nsq", 3, [128, 512], F32, st), Ring(b, "nrs", 2, [128, 512], F32, st))
            tmp_ring = Ring(b, "ntmp", 3, [128, 512], F32, st)
            for gi, (g0, N) in enumerate(GROUPS):
                j = 2 if gi == 0 else bi
                rs, rk = self.rms_rstd(rings, self.xT, g0, N, gi, 1.0 / D, self.pb[gi % 2], "pb%d" % (gi % 2))
                for kc in range(8):
                    tp, tk = tmp_ring.next()
                    b.stt("dve", tp[:, :N], self.xT[:, kc, g0:g0 + N], self.modv[:, l, j, 0, kc:kc + 1], rs[:, :N],
                          ALU.mult, ALU.mult, r=[("xT", gi), rk, "modv"], w=[tk])
                    b.act(self.hT[:, kc, g0:g0 + N], tp[:, :N], AF.Identity, r=[tk, "modv"], w=[("hT", gi)],
                          bias=self.modv[:, l, j, 1, kc:kc + 1])
            b.barrier()
            if self.dbg:
                b.dma("sp", self.hT_d, self.hT[:], r=[("hT", g) for g in range(5)], w=["hT_d"])

    def phase_attn(self, bi, l, kind):
        b = self
        ax = (kind == "ax")
        ki = 1 if ax else 0
        br = 2 if ax else 1
        scale = 0.125
        LOOK2 = 1
        with contextlib.ExitStack() as st:
            wt = b.sb("wqkv", [128, 8, 1408], BF16, st)
            tab = b.sb("axtab", [128, 2, S], F32, st)
            kT = b.sb("kT", [128, 2, TOK], BF16, st)
            Va = b.sb("Va", [128, NT, 2, 128], BF16, st)
            tf = Ring(b, "tf", 3, [128, 512], F32, st)
            tg = Ring(b, "tg", 4, [128, 512], F32, st)
            qT_ring = Ring(b, "qT", 6, [128, 512], BF16, st)
            sg_ring = Ring(b, "sg", 6, [128, 512], BF16, st)
            P_ring = Ring(b, "P", 4, [128, 512], BF16, st)
            og_ring = Ring(b, "og", 2, [128, 512], BF16, st)
            pbP = [(self.pb[0], "pb0"), (self.pb[1], "pb1")]
            pbM = (self.pb[2], "pb2")
            pbS = [(self.pb[3], "pb3"), (self.pb[4], "pb4"), (self.pb[5], "pb5"), (self.pb[2], "pb2")]
            pbO = [(self.pb[6], "pb6"), (self.pb[7], "pb7")]
            cnt = {"p": 0, "s": 0, "o": 0}

            def nextP():
                cnt["p"] += 1
                return pbP[cnt["p"] % 2]

            def nextS():
                cnt["s"] += 1
                return pbS[cnt["s"] % 4]

            def nextO():
                cnt["o"] += 1
                return pbO[cnt["o"] % 2]

            ck_ = self.ckeys[("s_attn", l, ki)]
            b.dma("sp", wt[:, :, 1024:1408], self.s_attn[l, ki][:, :, 1024:1408], r=ck_, w=["wqkv_kv"])
            b.dma("sp", wt[:, :, 0:512], self.s_attn[l, ki][:, :, 0:512], r=ck_, w=["wqkv_q"])
            b.dma("sp", wt[:, :, 512:1024], self.s_attn[l, ki][:, :, 512:1024], r=ck_, w=["wqkv_g"])
            b.dma("sp", tab[:], self.axtab_d, w=["axtab"])
            b.memset("dve", Va[:, :, :, 64:128], 1.0, w=["Va1"])
            WQ, WG, WK, WV = 0, 512, 1024, 1280

            def rope(src_ap, skey, g0, N, out_ap, okey):
                p0 = g0 - L
                bank, bk = pbM
                b.mm(bank[:, :N], lhsT=self.swap_f, rhs=src_ap, start=True, stop=True, r=[skey, "cst"], w=[bk])
                t1, k1 = tf.next()
                b.tt("dve", t1[:, :N], src_ap, tab[:, 0, p0:p0 + N], ALU.mult, r=[skey, "axtab"], w=[k1])
                t2, k2 = tf.next()
                b.tt("dve", t2[:, :N], bank[:, :N], tab[:, 1, p0:p0 + N], ALU.mult, r=[bk, "axtab"], w=[k2])
                b.tt("dve", out_ap, t1[:, :N], t2[:, :N], ALU.add, r=[k1, k2], w=[okey])

            def headnorm(bank, bk, N, gain_col):
                sq, sk = tf.next()
                b.act(sq[:, :N], bank[:, :N], AF.Square, r=[bk], w=[sk])
                mb, mk = pbM
                b.mm(mb[:, :N], lhsT=self.bones_f, rhs=sq[:, :N], start=True, stop=True, r=[sk, "cst"], w=[mk])
                rs, rk = tg.next()
                self.rstd_from_psum(rs[:, :N], rk, mb[:, :N], mk, 1.0 / 64)
                qn, qk = tg.next()
                b.stt("dve", qn[:, :N], bank[:, :N], gain_col, rs[:, :N], ALU.mult, ALU.mult, r=[bk, rk, "par"], w=[qk])
                return qn, qk

            for gi, (g0, N) in enumerate(GROUPS):
                for kv in range(2):
                    bank, bk = nextP()
                    for kc in range(8):
                        b.mm(bank[:, :N], lhsT=wt[:, kc, WK + kv * 128:WK + (kv + 1) * 128], rhs=self.hT[:, kc, g0:g0 + N],
                             start=(kc == 0), stop=(kc == 7), r=["wqkv_kv", ("hT", gi)], w=[bk])
                    okey = ("kT", kv, gi)
                    if ax:
                        kn, kk = headnorm(bank, bk, N, self.pcol(P_KG + l))
                    else:
                        kn, kk = tg.next()
                        b.cp("act", kn[:, :N], bank[:, :N], r=[bk], w=[kk])
                    if gi == 0:
                        b.cp("dve", kT[:, kv, g0:g0 + N], kn[:, :N], r=[kk], w=[okey])
                    else:
                        rope(kn[:, :N], kk, g0, N, kT[:, kv, g0:g0 + N], okey)
            for t in range(NT):
                bank, bk = nextP()
                gi = self.gi_of_tile(t)
                for kc in range(8):
                    b.mm(bank[:, 0:128], lhsT=self.hT[:, kc, t * 128:(t + 1) * 128], rhs=wt[:, kc, WV:WV + 128],
                         start=(kc == 0), stop=(kc == 7), r=["wqkv_kv", ("hT", gi)], w=[bk])
                b.cp("act" if t % 2 == 0 else "dve", Va[:, t, :, 0:64], bank[:, 0:128].rearrange("p (g d) -> p g d", g=2),
                     r=[bk], w=[("Va", t)])

            for gi, (g0, N) in enumerate(GROUPS):
                if gi == 0:
                    items = [(0, 0, N, None), (1, 0, N, None)]
                elif ax:
                    items = [(t, 0, N, None) for t in range(NT)]
                else:
                    items = [(0, 0, N, None), (1, 0, N, None)]
                    i0 = 4 * (gi - 1)
                    for kb in range(i0 - 1, i0 + 5):
                        if kb < 0 or kb > 15:
                            continue
                        qlo = max(kb - 1, i0)
                        qhi = min(kb + 1, i0 + 3)
                        items.append((2 + kb, (qlo - i0) * 128, (qhi - i0 + 1) * 128, (qlo - kb + 1) * 128))
                n_it = len(items)
                qTs, sgs = [], []
                for qc in range(4):
                    bank, bk = nextP()
                    for kc in range(8):
                        b.mm(bank[:, :N], lhsT=wt[:, kc, WQ + qc * 128:WQ + (qc + 1) * 128], rhs=self.hT[:, kc, g0:g0 + N],
                             start=(kc == 0), stop=(kc == 7), r=["wqkv_q", ("hT", gi)], w=[bk])
                    if ax:
                        qn, qk = headnorm(bank, bk, N, self.pcol(P_QG + l))
                    else:
                        qn, qk = tg.next()
                        b.cp("act", qn[:, :N], bank[:, :N], r=[bk], w=[qk])
                    qT, qTk = qT_ring.next()
                    if gi == 0:
                        b.cp("dve", qT[:, :N], qn[:, :N], r=[qk], w=[qTk])
                    else:
                        rope(qn[:, :N], qk, g0, N, qT[:, :N], qTk)
                    qTs.append((qT, qTk))
                    bank, bk = nextP()
                    for kc in range(8):
                        b.mm(bank[:, :N], lhsT=wt[:, kc, WG + qc * 128:WG + (qc + 1) * 128], rhs=self.hT[:, kc, g0:g0 + N],
                             start=(kc == 0), stop=(kc == 7), r=["wqkv_g", ("hT", gi)], w=[bk])
                    e1, ek = tg.next()
                    b.act(e1[:, :N], bank[:, :N], AF.Exp, r=[bk], w=[ek], scale=-1.0)
                    b.act(e1[:, :N], e1[:, :N], AF.Ln, r=[ek], w=[ek], bias=1.0)
                    b.act(e1[:, :N], e1[:, :N], AF.Exp, r=[ek], w=[ek], scale=-1.0)
                    sg, sgk = sg_ring.next()
                    b.tt("dve", sg[:, :N], bank[:, :N], e1[:, :N], ALU.mult, r=[bk, ek], w=[sgk])
                    sgs.append((sg, sgk))
                for qc in range(4):
                    kv = qc // 2
                    qT, qTk = qTs[qc]
                    sg, sgk = sgs[qc]
                    og, ogk = og_ring.next()
                    obs = [nextO(), nextO()]
                    pend = []

                    def tail(pds, idx):
                        for hh, (sb_, sbk_, t_, c0_, c1_, m0_) in enumerate(pds):
                            ob, obk = obs[hh]
                            n_ = c1_ - c0_
                            Pt, Pk = P_ring.next()
                            b.act(Pt[:, :n_], sb_[:, :n_], AF.Exp, r=[sbk_], w=[Pk], scale=scale)
                            if m0_ is not None:
                                b.tt("dve", Pt[:, :n_], Pt[:, :n_], self.wmask[:, m0_:m0_ + n_], ALU.mult,
                                     r=[Pk, "wmask"], w=[Pk])
                            b.mm(ob[:, c0_:c1_], lhsT=Va[:, t_, kv, :], rhs=Pt[:, :n_], start=(idx == 0), stop=(idx == n_it - 1),
                                 r=[("Va", t_), "Va1", Pk], w=[obk])

                    for idx, (t, c0, c1, m0) in enumerate(items):
                        n = c1 - c0
                        ents = []
                        for hh in range(2):
                            ph = hh * 64
                            sbank, sbk = nextS()
                            b.mm(sbank[:, :n], lhsT=kT[ph:ph + 64, kv, t * 128:(t + 1) * 128], rhs=qT[ph:ph + 64, c0:c1],
                                 start=True, stop=True, r=[("kT", kv, self.gi_of_tile(t)), qTk], w=[sbk])
                            ents.append((sbank, sbk, t, c0, c1, m0))
                        pend.append(ents)
                        if len(pend) > LOOK2:
                            tail(pend.pop(0), idx - LOOK2)
                    base = n_it - len(pend)
                    for i_, pd in enumerate(pend):
                        tail(pd, base + i_)
                    for hh in range(2):
                        head = 2 * qc + hh
                        ph = hh * 64
                        ob, obk = obs[hh]
                        rec, rck = tg.next()
                        if ax:
                            b.act(rec[64:128, :N], ob[64:128, :N], AF.Ln, r=[obk], w=[rck])
                        else:
                            b.act(rec[64:128, :N], ob[64:128, :N], AF.Ln, r=[obk, "esink"], w=[rck],
                                  bias=self.esink[64:128, l * 8 + head:l * 8 + head + 1])
                        b.act(rec[64:128, :N], rec[64:128, :N], AF.Exp, r=[rck], w=[rck], scale=-1.0)
                        u, uk = tf.next()
                        b.tt("dve", u[64:128, :N], ob[0:64, :N], sg[ph:ph + 64, :N], ALU.mult, r=[obk, sgk], w=[uk])
                        b.tt("dve", og[ph:ph + 64, :N], u[64:128, :N], rec[64:128, :N], ALU.mult, r=[uk, rck], w=[ogk])
                    b.dma("sp", self.og_scr[br * 4 + qc][:, g0:g0 + N], og[:, :N], r=[ogk], w=[("ogs", br * 4 + qc, gi)])
            b.barrier()

    def phase_ret(self, bi, l):
        b = self
        NSK = 6
        with contextlib.ExitStack() as st:
            rtab = b.sb("rtab", [128, 2, 20, 64], F32, st)
            b.dma("sp", rtab[:], self.rtab_d, w=["rtab"])
            wr_bufs = [b.sb("wr", [128, 8, 512], BF16, st) for _ in range(2)]
            out_f = b.sb("outf", [128, NT, 128], F32, st)
            sgT = b.sb("sgT", [128, TOK], BF16, st)
            ogT = b.sb("ogT", [128, TOK], BF16, st)
            R = b.sb("R", [128, 128], F32, st)
            Rb = b.sb("Rb", [128, 128], BF16, st)
            xqk_r = Ring(b, "xqk", NSK, [128, 2, 64, 2], F32, st)
            ra = Ring(b, "ra", NSK, [128, 4, 2, 64], F32, st)
            qkr_r = Ring(b, "qkr", NSK, [128, 2, 64, 2], BF16, st)
            v_r = Ring(b, "vsb", NSK, [128, 128], BF16, st)
            kd_r = Ring(b, "kd", NSK, [128, 128], BF16, st)
            qkT_r = Ring(b, "qkT", NSK, [128, 256], BF16, st)
            A_r = Ring(b, "A", NSK, [128, 128], BF16, st)
            ti_r = Ring(b, "ti", NSK, [128, 128], F32, st)
            tot_r = Ring(b, "tot", NSK, [128, 128], F32, st)
            on_r = Ring(b, "on", NSK, [128, 128], BF16, st)
            ssq_r = Ring(b, "ssq", NSK, [128, 2], F32, st)
            junk_r = Ring(b, "junk", 2, [128, 128], F32, st)
            gt_r = Ring(b, "gt", 2, [128, 512], F32, st)
            kdb = b.sb("kdb", [128, 8, 128], F32, st)
            for i8 in range(8):
                b.ts("dve", kdb[:, i8, :], self.ones_f, self.decs[:, 1, i8:i8 + 1], None, ALU.mult, None,
                     r=["cst", "decs"], w=["kdb"])
            import os
            for h in range(int(os.environ.get("RET_H", "4"))):
                wr = wr_bufs[h % 2]
                wrk_ = "wr%d" % (h % 2)
                if h == 0:
                    b.dma("sp", wr[:], self.s_ret[l, 0], r=self.ckeys[("s_ret", l, 0)], w=[wrk_])
                if h + 1 < 4:
                    b.dma("sp", wr_bufs[(h + 1) % 2][:], self.s_ret[l, h + 1], r=self.ckeys[("s_ret", l, h + 1)],
                          w=["wr%d" % ((h + 1) % 2)])
                for gi, (g0, N) in enumerate(GROUPS):
                    bank, bk = self.pb[gi % 2], "pb%d" % (gi % 2)
                    for kc in range(8):
                        b.mm(bank[:, :N], lhsT=wr[:, kc, 384:512], rhs=self.hT[:, kc, g0:g0 + N],
                             start=(kc == 0), stop=(kc == 7), r=[wrk_, ("hT", gi)], w=[bk])
                    e1, ek = gt_r.next()
                    b.act(e1[:, :N], bank[:, :N], AF.Exp, r=[bk], w=[ek], scale=-1.0)
                    b.act(e1[:, :N], e1[:, :N], AF.Ln, r=[ek], w=[ek], bias=1.0)
                    b.act(e1[:, :N], e1[:, :N], AF.Exp, r=[ek], w=[ek], scale=-1.0)
                    b.tt("dve", sgT[:, g0:g0 + N], bank[:, :N], e1[:, :N], ALU.mult, r=[bk, ek], w=[("sgT", gi)])
                for d in range(int(os.environ.get("RET_D", "2"))):
                    di = d * 4 + h
                    qdec = self.decs[:, 0, di:di + 1]
                    kdec = self.decs[:, 1, di:di + 1]
                    g128 = self.decs[:, 2, di:di + 1]
                    order = list(range(NT)) if d == 0 else [1, 0] + list(range(NT - 1, 1, -1))
                    b.memset("dve", R[:], 0.0, w=["R"])
                    b.memset("dve", Rb[:], 0.0, w=["Rb"])
                    stt_ = {}

                    def stage0(i):
                        t = order[i]
                        gi = self.gi_of_tile(t)
                        bank, bk = self.pb[i % 2], "pb%d" % (i % 2)
                        for kc in range(8):
                            b.mm(bank[:, 0:384], lhsT=self.hT[:, kc, t * 128:(t + 1) * 128], rhs=wr[:, kc, 0:384],
                                 start=(kc == 0), stop=(kc == 7), r=[wrk_, ("hT", gi)], w=[bk])
                        xq, xk = xqk_r.next()
                        b.cp("act", xq[:].rearrange("p a b c -> p (a b c)"), bank[:, 0:256], r=[bk], w=[xk])
                        vs, vk = v_r.next()
                        b.cp("act", vs[:], bank[:, 256:384], r=[bk], w=[vk])
                        tp = t if (d == 0 or t >= 2) else 18 + t
                        cosb = rtab[:, 0, tp:tp + 1, :].to_broadcast([128, 2, 64])
                        sinb = rtab[:, 1, tp:tp + 1, :].to_broadcast([128, 2, 64])
                        x0 = xq[:, :, :, 0]
                        x1 = xq[:, :, :, 1]
                        pr, pk = ra.next()
                        b.tt("dve", pr[:, 0], x0, cosb, ALU.mult, r=[xk, "rtab"], w=[(pk, 0)])
                        b.tt("dve", pr[:, 1], x1, sinb, ALU.mult, r=[xk, "rtab"], w=[(pk, 1)])
                        b.tt("dve", pr[:, 2], x0, sinb, ALU.mult, r=[xk, "rtab"], w=[(pk, 2)])
                        b.tt("dve", pr[:, 3], x1, cosb, ALU.mult, r=[xk, "rtab"], w=[(pk, 3)])
                        qr, qrk = qkr_r.next()
                        if d == 0:
                            b.tt("dve", qr[:, :, :, 0], pr[:, 0], pr[:, 1], ALU.subtract, r=[(pk, 0), (pk, 1)], w=[(qrk, 0)])
                            b.tt("pool", qr[:, :, :, 1], pr[:, 2], pr[:, 3], ALU.add, r=[(pk, 2), (pk, 3)], w=[(qrk, 1)])
                        else:
                            b.tt("dve", qr[:, :, :, 0], pr[:, 0], pr[:, 1], ALU.add, r=[(pk, 0), (pk, 1)], w=[(qrk, 0)])
                            b.tt("pool", qr[:, :, :, 1], pr[:, 3], pr[:, 2], ALU.subtract, r=[(pk, 2), (pk, 3)], w=[(qrk, 1)])
                        kd, kdk = kd_r.next()
                        b.tt("pool", kd[:], qr[:, 1].rearrange("p a b -> p (a b)"), kdb[:, di, :], ALU.mult,
                             r=[(qrk, 0), (qrk, 1), "kdb"], w=[kdk])
                        stt_[i] = dict(t=t, gi=gi, vs=vs, vk=vk, qr=qr, qrk=qrk, kd=kd, kdk=kdk)

                    def stage1(i):
                        s_ = stt_[i]
                        qr = s_["qr"]
                        bank, bk = self.pb[2 + i % 2], "pb%d" % (2 + i % 2)
                        for n_ in range(2):
                            b.mm(bank[:, n_ * 128:(n_ + 1) * 128], lhsT=qr[:, n_].rearrange("p a b -> p (a b)"), rhs=self.ident_bf[:],
                                 start=True, stop=True, r=[(s_["qrk"], 0), (s_["qrk"], 1), "identbf"], w=[bk])
                        qkT, qkTk = qkT_r.next()
                        b.cp("act", qkT[:], bank[:, 0:256], r=[bk], w=[qkTk])
                        s_["qkT"], s_["qkTk"] = qkT, qkTk

                    def stage2(i):
                        s_ = stt_[i]
                        qkT = s_["qkT"]
                        bank, bk = self.pb[4], "pb4"
                        c = (i % 2) * 128
                        b.mm(bank[:, c:c + 128], lhsT=qkT[:, 128:256], rhs=qkT[:, 0:128], start=True, stop=True,
                             r=[s_["qkTk"]], w=[bk])
                        A, Ak = A_r.next()
                        b.tt("dve", A[:], bank[:, c:c + 128], self.Dm[:, di, :], ALU.mult, r=[bk, "Dm"], w=[Ak])
                        s_["A"], s_["Ak"] = A, Ak

                    def stage3(i):
                        s_ = stt_[i]
                        t = s_["t"]
                        bank, bk = self.pb[5], "pb5"
                        c = (i % 2) * 256
                        b.mm(bank[:, c:c + 128], lhsT=s_["A"][:], rhs=s_["vs"][:], start=True, stop=True,
                             r=[s_["Ak"], s_["vk"]], w=[bk])
                        cb_, cbk = self.pb[7], "pb7"
                        cc = (i % 2) * 128
                        b.mm(cb_[:, cc:cc + 128], lhsT=s_["qkT"][:, 0:128], rhs=Rb[:], start=True, stop=True,
                             r=[s_["qkTk"], "Rb"], w=[cbk])
                        kb_, kbk = self.pb[6], "pb6"
                        ck = (i % 2) * 128
                        b.mm(kb_[:, ck:ck + 128], lhsT=s_["kd"][:], rhs=s_["vs"][:], start=True, stop=True,
                             r=[s_["kdk"], s_["vk"]], w=[kbk])
                        b.stt("dve", R[:], R[:], g128, kb_[:, ck:ck + 128], ALU.mult, ALU.add, r=["R", kbk, "decs"], w=["R"])
                        b.cp("act", Rb[:], R[:], r=["R"], w=["Rb"])
                        ti, tik = ti_r.next()
                        b.cp("act", ti[:], bank[:, c:c + 128], r=[bk], w=[tik])
                        if d == 0:
                            b.stt("dve", out_f[:, t, :], cb_[:, cc:cc + 128], qdec, ti[:], ALU.mult, ALU.add,
                                  r=[cbk, tik, "decs"], w=[("outf", t)])
                        else:
                            tot, totk = tot_r.next()
                            b.stt("dve", tot[:], cb_[:, cc:cc + 128], qdec, ti[:], ALU.mult, ALU.add,
                                  r=[cbk, tik, "decs"], w=[totk])
                            b.tt("dve", tot[:], tot[:], out_f[:, t, :], ALU.add, r=[totk, ("outf", t)], w=[totk])
                            s_["tot"], s_["totk"] = tot, totk

                    def stage4(i):
                        s_ = stt_[i]
                        t = s_["t"]
                        gi = s_["gi"]
                        tot, totk = s_["tot"], s_["totk"]
                        ssq, sqk = ssq_r.next()
                        b.memset("dve", ssq[:], 0.0, w=[sqk])
                        jk, jkk = junk_r.next()
                        b.act(jk[:], tot[:], AF.Square, r=[totk, sqk], w=[jkk, sqk], accum_out=ssq[:, 0:1])
                        self.rstd_from_psum(ssq[:, 1:2], sqk, ssq[:, 0:1], sqk, 1.0 / 128)
                        on, onk = on_r.next()
                        b.ts("dve", on[:], tot[:], ssq[:, 1:2], None, ALU.mult, None, r=[totk, sqk], w=[onk])
                        bank, bk = self.pb[7], "pb7"
                        c = 256 + (i % 2) * 128
                        b.mm(bank[:, c:c + 128], lhsT=on[:], rhs=self.ident_bf[:], start=True, stop=True, r=[onk, "identbf"],
                             w=[bk])
                        b.tt("dve", ogT[:, t * 128:(t + 1) * 128], bank[:, c:c + 128], sgT[:, t * 128:(t + 1) * 128], ALU.mult,
                             r=[bk, ("sgT", gi)], w=[("ogT", gi)])

                    order = order[:int(os.environ.get("RET_N", "18"))]
                    n = len(order)
                    nst = 5 if d == 1 else 4
                    nst = min(nst, int(os.environ.get("RET_S", "5")))
                    for step in range(n + nst - 1):
                        if step < n:
                            stage0(step)
                        if nst > 1 and 0 <= step - 1 < n:
                            stage1(step - 1)
                        if nst > 2 and 0 <= step - 2 < n:
                            stage2(step - 2)
                        if nst > 3 and 0 <= step - 3 < n:
                            stage3(step - 3)
                        if nst > 4 and d == 1 and 0 <= step - 4 < n:
                            stage4(step - 4)
                            stt_.pop(step - 4)
                for gi, (g0, N) in enumerate(GROUPS):
                    b.dma("sp", self.og_scr[h][:, g0:g0 + N], ogT[:, g0:g0 + N], r=[("ogT", gi)], w=[("ogs", h, gi)])
            b.barrier()

    def phase_merge(self, bi, l):
        b = self
        SG = [[0, 1], [2, 3], [4]]
        with contextlib.ExitStack() as st:
            ogg = b.sb("ogg", [128, 12, 1024], BF16, st)
            mer = b.sb("mer", [128, 8, 1024], BF16, st)
            wout = b.sb("wout", [128, 8, 1024], BF16, st)
            wm_r = Ring(b, "wm", 2, [128, 4608], BF16, st)
            m_r = Ring(b, "mm", 2, [128, 512], F32, st)
            acc_r = Ring(b, "acc", 2, [128, 512], F32, st)
            tmp_r = Ring(b, "mtmp", 2, [128, 512], F32, st)
            b.dma("sp", wout[:], self.s_out[l], r=self.ckeys[("s_out", l)], w=["wout"])
            cnt = {"b": 0, "m": 0, "o": 0}
            for gis in SG:
                t0 = GROUPS[gis[0]][0]
                nsg = sum(GROUPS[g][1] for g in gis)
                for c in range(12):
                    b.dma("sp", ogg[:, c, 0:nsg], self.og_scr[c][:, t0:t0 + nsg],
                          r=[("ogs", c, gi) for gi in gis], w=[("ogg", c)])
                for dc in range(8):
                    wm, wmk = wm_r.next()
                    b.dma("sp", wm[:], self.s_mrg[l, dc], r=self.ckeys[("s_mrg", l, dc)], w=[wmk])
                    wmg = wm[:, 0:3072].rearrange("p (kc b n) -> p kc b n", kc=8, b=3)
                    wp = wm[:, 3072:4608].rearrange("p (b kc n) -> p b kc n", b=3, kc=4)
                    for gi in gis:
                        g0, N = GROUPS[gi]
                        o0 = g0 - t0
                        acc, ack = acc_r.next()
                        for br in range(3):
                            cnt["b"] += 1
                            bb_, bbk = self.pb[cnt["b"] % 3], "pb%d" % (cnt["b"] % 3)
                            for kc in range(4):
                                b.mm(bb_[:, :N], lhsT=wp[:, br, kc, :], rhs=ogg[:, br * 4 + kc, o0:o0 + N],
                                     start=(kc == 0), stop=(kc == 3), r=[wmk, ("ogg", br * 4 + kc)], w=[bbk])
                            cnt["m"] += 1
                            mb_, mbk = self.pb[3 + cnt["m"] % 3], "pb%d" % (3 + cnt["m"] % 3)
                            for kc in range(8):
                                b.mm(mb_[:, :N], lhsT=wmg[:, kc, br, :], rhs=self.hT[:, kc, g0:g0 + N],
                                     start=(kc == 0), stop=(kc == 7), r=[wmk, ("hT", gi)], w=[mbk])
                            m, mk = m_r.next()
                            b.act(m[:, :N], mb_[:, :N], AF.Sigmoid, r=[mbk], w=[mk])
                            if br == 0:
                                b.tt("dve", acc[:, :N], bb_[:, :N], m[:, :N], ALU.mult, r=[bbk, mk], w=[ack])
                            else:
                                tp, tk = tmp_r.next()
                                b.tt("dve", tp[:, :N], bb_[:, :N], m[:, :N], ALU.mult, r=[bbk, mk], w=[tk])
                                if br == 1:
                                    b.tt("dve", acc[:, :N], acc[:, :N], tp[:, :N], ALU.add, r=[ack, tk], w=[ack])
                                else:
                                    b.tt("dve", mer[:, dc, o0:o0 + N], acc[:, :N], tp[:, :N], ALU.add,
                                         r=[ack, tk], w=[("mer", dc)])
                for do in range(8):
                    for gi in gis:
                        g0, N = GROUPS[gi]
                        o0 = g0 - t0
                        j = 2 if gi == 0 else bi
                        cnt["o"] += 1
                        ob, obk = self.pb[6 + cnt["o"] % 2], "pb%d" % (6 + cnt["o"] % 2)
                        for dc in range(8):
                            b.mm(ob[:, :N], lhsT=wout[:, dc, do * 128:(do + 1) * 128], rhs=mer[:, dc, o0:o0 + N],
                                 start=(dc == 0), stop=(dc == 7), r=["wout", ("mer", dc)], w=[obk])
                        b.stt("dve", self.xT[:, do, g0:g0 + N], ob[:, :N], self.modv[:, l, j, 2, do:do + 1], self.xT[:, do, g0:g0 + N],
                              ALU.mult, ALU.add, r=[obk, "modv", ("xT", gi)], w=[("xT", gi)])
            b.barrier()


def _const_tables():
    f32 = np.float32
    ident = np.eye(128, dtype=f32)
    swap = np.zeros((128, 128), f32)
    for m in range(128):
        swap[m ^ 1, m] = 1.0
    ones = np.ones((128, 128), f32)
    bones = np.zeros((128, 128), f32)
    bones[0:64, 0:64] = 1.0
    bones[64:128, 64:128] = 1.0
    cst = np.stack([ident, swap, ones, bones], axis=1)
    p = np.arange(S)
    rows = (p // 64).astype(f32)
    cols = (p % 64).astype(f32)
    freqs = (np.float32(10000.0) ** (-np.arange(16, dtype=f32) / np.float32(16))).astype(f32)
    ang = np.concatenate([rows[:, None] * freqs[None], cols[:, None] * freqs[None]], axis=-1).astype(f32)
    q = np.arange(128)
    i_of_q = (q % 64) // 2
    sign = np.where((q % 2) == 0, -1.0, 1.0)
    a = ang.astype(np.float64)[:, i_of_q].T
    axtab = np.stack([np.cos(a), np.sin(a) * sign[:, None]], axis=1).astype(f32)
    theta = (np.float32(10000.0) ** (-np.linspace(0.0, 1.0, 64, dtype=f32))).astype(f32)
    pos = np.arange(20 * 128, dtype=f32)
    ra = (pos[:, None] * theta[None]).astype(f32).astype(np.float64)
    rc = np.cos(ra).reshape(20, 128, 64).transpose(1, 0, 2)
    rs = np.sin(ra).reshape(20, 128, 64).transpose(1, 0, 2)
    rtab = np.stack([rc, rs], axis=1).astype(f32)
    aa = np.arange(128)[:, None]
    bb = np.arange(128)[None, :]
    wmask = np.concatenate([(aa <= bb), np.ones((128, 128), bool), (bb <= aa)], axis=1).astype(f32)
    s_ = np.arange(128)[:, None]
    t_ = np.arange(128)[None, :]
    sc = np.float32(128 ** -0.5)
    dtab = np.stack([np.maximum(t_ - s_, 0), np.maximum(s_ - t_, 0), (t_ >= s_) * sc, (s_ > t_) * sc], axis=1).astype(f32)
    return cst, axtab, rtab, wmask, dtab


def _params(core, c, c_ctx, b_mod, norm_w, final_norm_w, ax_q_gain, ax_k_gain, win_sink, ret_decay_fwd, ret_decay_bwd):
    f32 = np.float32
    par = np.zeros((128, NPAR), f32)
    cs = np.stack([c[2 * core], c[2 * core + 1], c_ctx], axis=0)
    par[:, P_CT:P_CT + 24] = cs.reshape(3, 8, 128).transpose(2, 1, 0).reshape(128, 24)
    par[:, P_BMOD:P_BMOD + 96] = b_mod.reshape(DEPTH, 24, 128).transpose(2, 0, 1).reshape(128, 96)
    par[:, P_NORMW:P_NORMW + 32] = norm_w.reshape(DEPTH, 8, 128).transpose(2, 0, 1).reshape(128, 32)
    par[:, P_FNW:P_FNW + 8] = final_norm_w.reshape(8, 128).T
    idx = np.arange(128) % 64
    par[:, P_QG:P_QG + 4] = ax_q_gain[:, idx].T
    par[:, P_KG:P_KG + 4] = ax_k_gain[:, idx].T
    par[:, P_SINK:P_SINK + 32] = np.broadcast_to(win_sink.reshape(1, 32), (128, 32))
    dec = np.stack([ret_decay_fwd, ret_decay_bwd], axis=1)
    par[:, P_DEC:P_DEC + 32] = np.broadcast_to(dec.reshape(1, 32), (128, 32))
    par[:, P_EPS] = EPS
    pp = np.arange(128, dtype=f32)
    pexp = np.zeros((128, 3, 8), f32)
    pexp[:, 0, 0:4] = (pp + 1)[:, None]
    pexp[:, 0, 4:8] = (128 - pp)[:, None]
    pexp[:, 1, 0:4] = (127 - pp)[:, None]
    pexp[:, 1, 4:8] = pp[:, None]
    pexp[:, 2, :] = 128.0
    par[:, P_PEXP:P_PEXP + 24] = pexp.reshape(128, 24)
    return par


_CACHE = {}


def _get_prog(**kw):
    key = tuple(sorted(kw.items()))
    if key not in _CACHE:
        _CACHE[key] = Prog(**kw)
    return _CACHE[key]


def _run(inputs, **kw):
    f32 = np.float32
    g = lambda k: np.asarray(inputs[k], dtype=f32)
    x, c, ctx, c_ctx = g("x"), g("c"), g("ctx"), g("c_ctx")
    w_in = np.array(g("w_in"), dtype=f32, copy=True)
    perm = np.concatenate([np.arange(0, 128, 2), np.arange(1, 128, 2)])
    for c0 in ():
        for h in range(4):
            blk = w_in[:, :, c0 + h * 128:c0 + (h + 1) * 128].copy()
            w_in[:, :, c0 + h * 128:c0 + (h + 1) * 128] = blk[:, :, perm]
    cst, axtab, rtab, wmask, dtab = _const_tables()
    wk = w_in[:, :, C_WK:C_WK + 128]
    ak = w_in[:, :, C_AK:C_AK + 128]
    w_kd = np.ascontiguousarray(np.concatenate(
        [wk[..., 0:64], wk[..., 0:64], wk[..., 64:128], wk[..., 64:128],
         ak[..., 0:64], ak[..., 0:64], ak[..., 64:128], ak[..., 64:128]], axis=-1))
    w_p = np.ascontiguousarray(np.stack([g("w_proj_ret"), g("w_proj_win"), g("w_proj_ax")], axis=1))
    w_mod = np.ascontiguousarray(g("w_mod"))
    w_out = np.ascontiguousarray(g("w_out"))
    prog = _get_prog(**kw)
    in_maps = []
    for core in range(NCORES):
        xin = np.ascontiguousarray(np.concatenate([ctx[2 * core:2 * core + 2], x[2 * core:2 * core + 2]], axis=1))
        par = _params(core, c, c_ctx, g("b_mod"), g("norm_w"), g("final_norm_w"), g("ax_q_gain"), g("ax_k_gain"),
                      g("win_sink"), g("ret_decay_fwd"), g("ret_decay_bwd"))
        in_maps.append({"xin": xin, "par": par, "cst": cst, "axtab": axtab, "rtab": rtab, "wmask": wmask, "dtab": dtab,
                        "w_in": w_in, "w_kd": w_kd, "w_mod": w_mod, "w_p": w_p, "w_out": w_out})
    res = run_bass_kernel_spmd(prog.nc, in_maps, core_ids=list(range(NCORES)))
    global LAST_RES
    LAST_RES = res
    out = np.concatenate([np.asarray(r["y"]) for r in res.results], axis=0)
    return out.astype(np.float32)


def kernel(**inputs):
    return _run(inputs)
```
